# Optimizing a Trainium2 kernel written in Bass

```python
import jax, jax.numpy as jnp
from jax import lax
import numpy as np

D_MODEL = 1024
BATCH = 16
SEQ = 4096
DEPTH = 1
DEC_BATCH = 32
DEC_SEQ = 32
PAST_LEN = 4096

CHUNK = 64
RET_HEADS = 4
RET_QK_DIM = 256
RET_V_DIM = D_MODEL // RET_HEADS
RET_QK_WIDTH = RET_HEADS * RET_QK_DIM
RET_WIDTH = RET_HEADS * RET_V_DIM
SB_HEADS = 8
SB_HEAD_DIM = D_MODEL // SB_HEADS
SB_WIDTH = SB_HEADS * SB_HEAD_DIM
SB_BLOCK = 128
D_FF = 4 * D_MODEL
ROPE_BASE = 10000.0
EPS = 1e-6
IN_WIDTHS = (RET_QK_WIDTH, RET_QK_WIDTH, RET_WIDTH, RET_WIDTH,
             SB_WIDTH, SB_WIDTH, SB_WIDTH, D_MODEL, D_MODEL)
IN_WIDTH = sum(IN_WIDTHS)

kernel_name = "retention_stickbreaking_parallel_streaming_step"


def rmsnorm(x, w):
    xf = x.astype(jnp.float32)
    y = xf * lax.rsqrt(jnp.mean(xf * xf, axis=-1, keepdims=True) + EPS)
    return (y * w.astype(jnp.float32)).astype(x.dtype)


def rotary(x, pos):
    half = x.shape[-1] // 2
    inv_freq = ROPE_BASE ** (-jnp.arange(half, dtype=jnp.float32) / half)
    ang = pos.astype(jnp.float32)[:, None] * inv_freq[None, :]
    cos = jnp.cos(ang)[None, :, None, :]
    sin = jnp.sin(ang)[None, :, None, :]
    xf = x.astype(jnp.float32)
    x1, x2 = xf[..., :half], xf[..., half:]
    return jnp.concatenate([x1 * cos - x2 * sin, x1 * sin + x2 * cos], axis=-1).astype(x.dtype)


def retention_log_gamma():
    return jnp.log1p(-jnp.exp2(-5.0 - jnp.arange(RET_HEADS, dtype=jnp.float32)))


def project(h, pos, w_in_l, q_norm_w, k_norm_w):
    B, T, _ = h.shape
    bounds = np.concatenate([[0], np.cumsum(IN_WIDTHS)])
    parts = [h @ w_in_l[:, int(bounds[i]):int(bounds[i + 1])] for i in range(len(IN_WIDTHS))]
    rq, rk, rv, rg, sq, sk, sv, ga, gb = parts
    rq = rotary(rq.reshape(B, T, RET_HEADS, RET_QK_DIM), pos)
    rk = rotary(rk.reshape(B, T, RET_HEADS, RET_QK_DIM), pos) * (RET_QK_DIM ** -0.5)
    rv = rv.reshape(B, T, RET_HEADS, RET_V_DIM)
    sq = rmsnorm(sq.reshape(B, T, SB_HEADS, SB_HEAD_DIM), q_norm_w)
    sk = rmsnorm(sk.reshape(B, T, SB_HEADS, SB_HEAD_DIM), k_norm_w)
    sv = sv.reshape(B, T, SB_HEADS, SB_HEAD_DIM)
    return rq, rk, rv, rg, sq, sk, sv, ga, gb


def retention_chunk(S, q, k, v, log_gamma):
    L = q.shape[1]
    idx = jnp.arange(L, dtype=jnp.float32)
    decay = jnp.exp(log_gamma[:, None, None] * jnp.abs(idx[:, None] - idx[None, :]))
    scores = jnp.einsum('bihd,bjhd->bhij', q, k) * decay[None]
    o_intra = jnp.einsum('bhij,bjhe->bihe', scores, v)
    q_decay = jnp.exp(log_gamma[None, :] * (idx[:, None] + 1.0))
    o_cross = jnp.einsum('bihd,bhde->bihe', q, S) * q_decay[None, :, :, None]
    k_decay = jnp.exp(log_gamma[None, :] * (L - 1.0 - idx[:, None]))
    S_new = (jnp.exp(log_gamma * L)[None, :, None, None] * S
             + jnp.einsum('bjhd,bjhe->bhde', k * k_decay[None, :, :, None], v))
    return S_new, o_intra + o_cross


def retention_prompt(q, k, v, log_gamma):
    B, T = q.shape[:2]
    nc = T // CHUNK
    to_chunks = lambda a: jnp.moveaxis(a.astype(jnp.float32).reshape(B, nc, CHUNK, *a.shape[2:]), 1, 0)
    S0 = jnp.zeros((B, RET_HEADS, RET_QK_DIM, RET_V_DIM), jnp.float32)
    S_fin, o = lax.scan(lambda S, xs: retention_chunk(S, xs[0], xs[1], xs[2], log_gamma),
                        S0, (to_chunks(q), to_chunks(k), to_chunks(v)))
    o = jnp.moveaxis(o, 0, 1).reshape(B, T, RET_HEADS, RET_V_DIM)
    return o, S_fin


def stick_breaking_block(q, k, v, q_pos, k_pos):
    z = jnp.einsum('bqhd,bkhd->bhqk', q, k).astype(jnp.float32) * (SB_HEAD_DIM ** -0.5)
    mask = (k_pos[None, :] < q_pos[:, None])[None, None]
    u = jnp.where(mask, jax.nn.log_sigmoid(-z), 0.0)
    after = lax.cumsum(u, axis=3, reverse=True) - u
    w = jnp.where(mask, jnp.exp(jax.nn.log_sigmoid(z) + after), 0.0)
    return jnp.einsum('bhqk,bkhd->bqhd', w.astype(v.dtype), v)


def stick_breaking_prompt(q, k, v):
    T = q.shape[1]
    pos = jnp.arange(T, dtype=jnp.int32)
    outs = []
    for b in range(T // SB_BLOCK):
        lo, hi = b * SB_BLOCK, (b + 1) * SB_BLOCK
        outs.append(stick_breaking_block(q[:, lo:hi], k[:, :hi], v[:, :hi], pos[lo:hi], pos[:hi]))
    return jnp.concatenate(outs, axis=1)


def merge(o_ret, rg, o_sb, ga, gb, ret_norm_w_l, w_out_l):
    B, T = o_ret.shape[:2]
    r = rmsnorm(o_ret, ret_norm_w_l).reshape(B, T, RET_WIDTH).astype(rg.dtype) * jax.nn.silu(rg)
    s = o_sb.reshape(B, T, SB_WIDTH)
    mix = jax.nn.sigmoid(ga) * r + jax.nn.sigmoid(gb) * s
    return mix @ w_out_l


def sq_relu_mlp(h, w_up_l, w_down_l):
    return jnp.square(jax.nn.relu(h @ w_up_l)) @ w_down_l


def setup_inputs(seed: int = 0) -> dict:
    key = jax.random.key(seed)
    ks = jax.random.split(key, 16)
    f32 = jnp.float32
    nrm = lambda k, shape, s: jax.random.normal(k, shape, f32) * s
    return {
        "x_prompt": nrm(ks[0], (BATCH, SEQ, D_MODEL), 1.0),
        "x_sample": nrm(ks[1], (DEC_BATCH, DEC_SEQ, D_MODEL), 1.0),
        "cache_sb_k": nrm(ks[2], (DEPTH, DEC_BATCH, PAST_LEN, SB_HEADS, SB_HEAD_DIM), 1.0),
        "cache_sb_v": nrm(ks[3], (DEPTH, DEC_BATCH, PAST_LEN, SB_HEADS, SB_HEAD_DIM), 1.0),
        "state_ret": nrm(ks[4], (DEPTH, DEC_BATCH, RET_HEADS, RET_QK_DIM, RET_V_DIM), 0.5),
        "norm_mix_w": 1.0 + nrm(ks[5], (DEPTH, D_MODEL), 0.01),
        "w_in": nrm(ks[6], (DEPTH, D_MODEL, IN_WIDTH), D_MODEL ** -0.5),
        "ret_norm_w": 1.0 + nrm(ks[7], (DEPTH, RET_HEADS, RET_V_DIM), 0.01),
        "sb_q_norm_w": 1.0 + nrm(ks[8], (DEPTH, SB_HEAD_DIM), 0.01),
        "sb_k_norm_w": 1.0 + nrm(ks[9], (DEPTH, SB_HEAD_DIM), 0.01),
        "w_out": nrm(ks[10], (DEPTH, D_MODEL, D_MODEL), D_MODEL ** -0.5),
        "norm_mlp_w": 1.0 + nrm(ks[11], (DEPTH, D_MODEL), 0.01),
        "w_up": nrm(ks[12], (DEPTH, D_MODEL, D_FF), D_MODEL ** -0.5),
        "w_down": nrm(ks[13], (DEPTH, D_FF, D_MODEL), D_FF ** -0.5),
    }


def reference(x_prompt, x_sample, cache_sb_k, cache_sb_v, state_ret, norm_mix_w, w_in,
              ret_norm_w, sb_q_norm_w, sb_k_norm_w, w_out, norm_mlp_w, w_up, w_down):
    log_gamma = retention_log_gamma()
    T = x_prompt.shape[1]
    L = x_sample.shape[1]
    past = cache_sb_k.shape[2]
    pos_p = jnp.arange(T, dtype=jnp.int32)
    pos_s = past + jnp.arange(L, dtype=jnp.int32)
    k_pos_s = jnp.arange(past + L, dtype=jnp.int32)
    xp, xs = x_prompt, x_sample
    kp_l, vp_l, sp_l, ks_l, vs_l, ss_l = [], [], [], [], [], []
    for l in range(DEPTH):
        hp = rmsnorm(xp, norm_mix_w[l])
        rq, rk, rv, rg, sq, sk, sv, ga, gb = project(hp, pos_p, w_in[l], sb_q_norm_w[l], sb_k_norm_w[l])
        o_ret, S_p = retention_prompt(rq, rk, rv, log_gamma)
        o_sb = stick_breaking_prompt(sq, sk, sv)
        xp = xp + merge(o_ret, rg, o_sb, ga, gb, ret_norm_w[l], w_out[l])
        xp = xp + sq_relu_mlp(rmsnorm(xp, norm_mlp_w[l]), w_up[l], w_down[l])
        kp_l.append(sk)
        vp_l.append(sv)
        sp_l.append(S_p.astype(state_ret.dtype))
        hs = rmsnorm(xs, norm_mix_w[l])
        rq, rk, rv, rg, sq, sk, sv, ga, gb = project(hs, pos_s, w_in[l], sb_q_norm_w[l], sb_k_norm_w[l])
        S_s, o_ret_s = retention_chunk(state_ret[l].astype(jnp.float32), rq.astype(jnp.float32),
                                       rk.astype(jnp.float32), rv.astype(jnp.float32), log_gamma)
        k_all = jnp.concatenate([cache_sb_k[l].astype(sk.dtype), sk], axis=1)
        v_all = jnp.concatenate([cache_sb_v[l].astype(sv.dtype), sv], axis=1)
        o_sb_s = stick_breaking_block(sq, k_all, v_all, pos_s, k_pos_s)
        xs = xs + merge(o_ret_s, rg, o_sb_s, ga, gb, ret_norm_w[l], w_out[l])
        xs = xs + sq_relu_mlp(rmsnorm(xs, norm_mlp_w[l]), w_up[l], w_down[l])
        ks_l.append(sk)
        vs_l.append(sv)
        ss_l.append(S_s.astype(state_ret.dtype))
    return (xp, xs, jnp.stack(kp_l), jnp.stack(vp_l), jnp.stack(sp_l),
            jnp.stack(ks_l), jnp.stack(vs_l), jnp.stack(ss_l))
```

```python
import numpy as np
import ml_dtypes
from contextlib import ExitStack
import concourse.bass as bass
import concourse.mybir as mybir
from concourse.bass_utils import run_bass_kernel_spmd

F32 = mybir.dt.float32
BF16 = mybir.dt.bfloat16
AF = mybir.ActivationFunctionType
ALU = mybir.AluOpType
BF = ml_dtypes.bfloat16

D = 1024
EPS = 1e-6
NCORES = 8
ENGS = ["pe", "act", "dve", "pool", "sp"]
EPOCH = 20000

C_NMIX, C_NMLP, C_RETW, C_QW, C_KW = 0, 1024, 2048, 3072, 3200
C_DP, C_DS, C_KDP, C_QDP, C_KDS, C_QDS = 3328, 3840, 4352, 4356, 4360, 4364
NCF = 4368
B_ID, B_TRI, B_NEG, B_MNEG = 0, 128, 256, 384
NCB = 512


class Slot:
    def __init__(self, h):
        self.h = h
        self.n = 0


class Buf:
    def __init__(self, t, slot=None):
        self.t = t
        self.w = None
        self.r = {}
        self.slot = slot

    def __getitem__(self, k):
        return self.t[k]


class Plan:
    def __init__(self, nc, stack):
        self.nc = nc
        self.stack = stack
        self.ops = {e: [] for e in ENGS}
        self.slots = []

    def slot(self):
        h = self.stack.enter_context(self.nc.semaphore(f"dq{len(self.slots)}"))
        s = Slot(h)
        self.slots.append(s)
        return s

    def add(self, eng, fn, reads=(), writes=(), slot=None, extra=()):
        deps = list(extra)
        for b in reads:
            if b.w is not None:
                deps.append(b.w)
            if getattr(b, "psum", False):
                for key, tkr in b.r.items():
                    if key != eng:
                        deps.append(tkr)
        for b in writes:
            if b.w is not None:
                deps.append(b.w)
            deps.extend(b.r.values())
        idx = len(self.ops[eng])
        if slot is not None:
            slot.n += 16
            assert slot.n < 60000
            tk = ("d", slot, slot.n)
        else:
            tk = ("e", eng, idx)
        for b in reads:
            b.r[tk[1]] = tk
        for b in writes:
            b.w = tk
            b.r = {}
        self.ops[eng].append([fn, deps, slot, False])
        return tk

    def emit(self):
        nc = self.nc
        for e in ENGS:
            for op in self.ops[e]:
                for d in op[1]:
                    if d[0] == "e" and not (d[1] == e and e == "pe"):
                        self.ops[d[1]][d[2]][3] = True
        val = {}
        nsem = {}
        for e in ENGS:
            c = 0
            val[e] = []
            for op in self.ops[e]:
                if op[3]:
                    c += 1
                val[e].append(c)
            nsem[e] = max(1, (c + EPOCH - 1) // EPOCH)
        sems = {e: [self.stack.enter_context(nc.semaphore(f"s_{e}{k}")) for k in range(nsem[e])]
                for e in ENGS}
        ops = self.ops

        def run2(e, eng):
            waited = {}
            for i, op in enumerate(ops[e]):
                for d in op[1]:
                    if d[0] == "e":
                        if d[1] == e and e == "pe":
                            continue
                        c = val[d[1]][d[2]]
                        key = d[1]
                        if waited.get(key, 0) >= c:
                            continue
                        waited[key] = c
                        eng.wait_ge(sems[d[1]][(c - 1) // EPOCH], (c - 1) % EPOCH + 1)
                    else:
                        key = id(d[1])
                        if waited.get(key, 0) >= d[2]:
                            continue
                        waited[key] = d[2]
                        eng.wait_ge(d[1].h, d[2])
                if op[0] is None:
                    continue
                ins = op[0](eng)
                if op[2] is not None:
                    ins.then_inc(op[2].h, 16)
                elif op[3]:
                    c = val[e][i]
                    ins.then_inc(sems[e][(c - 1) // EPOCH], 1)

        with nc.Block() as block:
            @block.tensor
            def _(t):
                run2("pe", t)

            @block.scalar
            def _(t):
                run2("act", t)

            @block.vector
            def _(t):
                run2("dve", t)

            @block.gpsimd
            def _(t):
                run2("pool", t)

            @block.sync
            def _(t):
                run2("sp", t)


def make_consts(T, PAST, LS):
    cf = np.zeros((128, NCF), np.float32)
    lg = np.log1p(-np.exp2(-5.0 - np.arange(4, dtype=np.float64)))
    i = np.arange(128)
    for h in range(4):
        g = lg[h]
        Dm = np.zeros((128, 128))
        for jj in range(128):
            for ii in range(128):
                cj, ci = jj // 64, ii // 64
                if cj == ci:
                    Dm[jj, ii] = np.exp(g * abs(ii - jj))
                elif cj < ci:
                    Dm[jj, ii] = np.exp(g * (ii - jj))
        qd = np.exp(g * (i + 1.0))
        cf[:, C_DP + h * 128:C_DP + (h + 1) * 128] = Dm / qd[None, :] / 16.0
        cf[:, C_KDP + h] = np.exp(g * (127.0 - i)) / 16.0
        cf[:, C_QDP + h] = qd
        Ds = np.zeros((128, 128))
        Ds[:LS, :LS] = np.exp(g * np.abs(i[:LS, None] - i[None, :LS]))
        qds = np.ones(128)
        qds[:LS] = np.exp(g * (i[:LS] + 1.0))
        cf[:, C_DS + h * 128:C_DS + (h + 1) * 128] = Ds / qds[None, :] / 16.0
        kds = np.zeros(128)
        kds[:LS] = np.exp(g * (LS - 1.0 - i[:LS])) / 16.0
        cf[:, C_KDS + h] = kds
        cf[:, C_QDS + h] = qds
    gamP = [float(np.exp(lg[h] * 128.0)) for h in range(4)]
    gamS = [float(np.exp(lg[h] * LS)) for h in range(4)]
    cb = np.zeros((128, NCB), np.float32)
    cb[:, B_ID:B_ID + 128] = np.eye(128)
    cb[:, B_TRI:B_TRI + 128] = -(i[:, None] >= i[None, :]).astype(np.float32)
    cb[:, B_NEG:B_NEG + 128] = -1.0
    cb[:, B_MNEG:B_MNEG + 128] = np.where(i[:, None] < i[None, :], 0.0, -30000.0)
    half = 128
    inv_freq = (10000.0 ** (-np.arange(half, dtype=np.float32) / half)).astype(np.float32)
    pos = np.concatenate([np.arange(T), PAST + np.arange(128)]).astype(np.float32)
    ang = (pos[None, :] * inv_freq[:, None]).astype(np.float32)
    cs = np.stack([np.cos(ang), np.sin(ang)], axis=1).astype(np.float32)
    return cf, cb.astype(BF), cs, gamP, gamS


def build(NP, T, NS, PAST, LS, SEG):
    nc = bass.Bass("TRN2", target_bir_lowering=False)
    stack = ExitStack()
    P = Plan(nc, stack)
    NG = SEG // 512
    NSEG = T // SEG
    TT = T + 128

    def dram(name, shape, dtype=F32, kind="ExternalInput"):
        return nc.dram_tensor(name, shape, dtype, kind=kind).ap()

    xp = dram("xp", [NP, T, D])
    xs = dram("xs", [NS, LS, D])
    ck = dram("ck", [NS, PAST, D])
    cv = dram("cv", [NS, PAST, D])
    st_in = dram("st", [NS, 4, 256, 256])
    w_in = dram("w_in", [D, 9216])
    w_out = dram("w_out", [D, D])
    w_up = dram("w_up", [D, 4096])
    w_dn = dram("w_dn", [4096, D])
    cf_d = dram("cf", [128, NCF])
    cb_d = dram("cb", [128, NCB], BF16)
    cs_d = dram("cs", [128, 2, TT])
    yp = dram("yp", [NP, T, D], kind="ExternalOutput")
    ys = dram("ys", [NS, LS, D], kind="ExternalOutput")
    kp = dram("kp", [NP, T, D], kind="ExternalOutput")
    vp = dram("vp", [NP, T, D], kind="ExternalOutput")
    sp_o = dram("sp", [NP, 4, 256, 256], kind="ExternalOutput")
    ks = dram("ks", [NS, LS, D], kind="ExternalOutput")
    vs = dram("vs", [NS, LS, D], kind="ExternalOutput")
    ss_o = dram("ss", [NS, 4, 256, 256], kind="ExternalOutput")
    winb = nc.dram_tensor("winb", [D, 9216], BF16).ap()
    woutb = nc.dram_tensor("woutb", [D, D], BF16).ap()
    wupb = nc.dram_tensor("wupb", [D, 4096], BF16).ap()
    wdnb = nc.dram_tensor("wdnb", [4096, D], BF16).ap()

    def SB(name, shape, dtype, slot=False):
        t = stack.enter_context(nc.sbuf_tensor("sb_" + name, shape, dtype))
        return Buf(t, P.slot() if slot else None)

    def PSB(name, shape, dtype):
        bb = Buf(stack.enter_context(nc.psum_tensor("pp_" + name, shape, dtype)))
        bb.psum = True
        return bb

    cf = SB("cf", [128, NCF], F32, True)
    cb = SB("cb", [128, NCB], BF16, True)
    hT = [SB(f"hT{g}", [128, 8, 512], BF16) for g in range(NG)]
    mixT = [SB(f"mx{g}", [128, 8, 512], BF16) for g in range(NG)]
    S32 = [SB(f"S{h}", [128, 2, 256], F32, True) for h in range(4)]
    Sbf = [SB(f"Sb{h}", [128, 2, 256], BF16) for h in range(4)]
    xb = [SB(f"xb{i}", [128, D], F32, True) for i in range(2)]
    hb = SB("hb", [128, D], BF16)
    junk = hb
    ssA = SB("ssA", [128, 4], F32)
    WBs = [SB(f"WB{i}", [128, 8, 1280], BF16, True) for i in range(2)]
    csb = SB("csb", [128, 2, 512], F32, True)
    qTr = SB("qTr", [128, 2, 512], BF16)
    kTr = SB("kTr", [128, 2, 512], BF16)
    k2 = SB("k2", [128, 256], BF16)
    vbf = SB("vbf", [128, 256], BF16)
    gsi = SB("gsi", [128, 256], F32)
    sga = SB("sga", [128, 256], F32)
    gate = SB("gate", [128, 256], F32)
    scd = SB("scd", [128, 128], BF16)
    ssB = SB("ssB", [128, 4], F32)
    rr2 = [SB(f"rr{i}", [128, 256], BF16) for i in range(2)]
    KMAX = max(T, PAST + 128)
    qTh = SB("qTh", [128, SEG], BF16)
    kTh = SB("kTh", [128, KMAX], BF16)
    vh = SB("vh", [128, KMAX // 128, 128], BF16)
    sgbT = SB("sgbT", [128, SEG], BF16)
    kst = [SB(f"kst{i}", [128, 4, 128], F32, True) for i in range(1)] * 2
    vst = [SB(f"vst{i}", [128, 4, 128], F32, True) for i in range(1)] * 2
    kstb = SB("kstb", [128, 4, 128], BF16)
    qn2 = [SB(f"qn{i}", [128, 128], BF16) for i in range(2)]
    knb2 = [SB(f"knb{i}", [128, 128], BF16) for i in range(2)]
    ssC2 = [SB(f"ssC{i}", [128, 4], F32) for i in range(2)]
    kout = [SB(f"kout{i}", [128, 128], F32, True) for i in range(2)]
    vout = [SB(f"vout{i}", [128, 128], F32, True) for i in range(2)]
    ssC = SB("ssC", [128, 4], F32)
    Eb = [SB(f"E{i}", [128, 512], F32) for i in range(2)]
    SPb = [SB(f"SP{i}", [128, 512], BF16) for i in range(2)]
    ARG = [SB(f"ARG{i}", [128, 512], F32) for i in range(2)]
    Wb = [SB(f"W{i}", [128, 512], BF16) for i in range(2)]
    Csum = SB("Csum", [128, 512], F32)
    otmp = SB("otmp", [128, 512], F32)
    rt = [Eb[0], Eb[1], ARG[0], ARG[1]]
    y2 = [SB(f"y2{i}", [128, D], F32, True) for i in range(4)]
    wup = [SB(f"wup{i}", [128, 8, 256], BF16, True) for i in range(2)]
    wdn = [SB(f"wdn{i}", [128, 2, D], BF16, True) for i in range(2)]
    uT = [SB(f"uT{i}", [128, 2, 512], BF16) for i in range(2)]
    sqb = [ARG[0], ARG[1]]
    PS = [PSB(f"ps{i}", [128, 512], F32) for i in range(7)]
    PST = PSB("pst", [128, 1024], BF16)
    wsl = P.slot()
    Wd = Buf(None, wsl)
    kpD = [Buf(None) for _ in range(NP)]
    vpD = [Buf(None) for _ in range(NP)]

    ident = lambda n=128: cb[0:n, B_ID:B_ID + n]

    def dma(q, out_ap, in_ap, reads, writes, slot):
        return P.add(q, lambda e: e.dma_start(out=out_ap, in_=in_ap), reads=reads, writes=writes, slot=slot)

    def mm(out_ap, lhsT, rhs, start, stop, reads, writes, skip=False):
        if skip:
            fn = lambda e: e.matmul(out_ap, lhsT=lhsT, rhs=rhs, start=start, stop=stop, skip_group_check=True)
        else:
            fn = lambda e: e.matmul(out_ap, lhsT=lhsT, rhs=rhs, start=start, stop=stop)
        return P.add("pe", fn, reads=reads, writes=writes)

    def tr(out_ap, in_ap, n, reads, writes):
        idn = ident(n)
        return P.add("pe", lambda e: e.transpose(out_ap, in_ap, idn), reads=list(reads) + [cb], writes=writes)

    def act(out_ap, in_ap, func, reads, writes, **kw):
        return P.add("act", lambda e: e.activation(out=out_ap, in_=in_ap, func=func, **kw), reads=reads, writes=writes)

    def stt(eng, out_ap, in0, scalar, in1, op0, op1, reads, writes):
        return P.add(eng, lambda e: e.scalar_tensor_tensor(out=out_ap, in0=in0, scalar=scalar, in1=in1, op0=op0, op1=op1),
                     reads=reads, writes=writes)

    def tt(eng, out_ap, in0, in1, op, reads, writes):
        return P.add(eng, lambda e: e.tensor_tensor(out=out_ap, in0=in0, in1=in1, op=op), reads=reads, writes=writes)

    def memset(eng, ap, v, writes):
        return P.add(eng, lambda e: e.memset(ap, v), writes=writes)

    def rstd_ops(ssb, n, reads_extra=()):
        act(ssb[:, 1:2], ssb[:, 0:1], AF.Ln, [], [ssb], scale=1.0 / n, bias=EPS)
        act(ssb[:, 2:3], ssb[:, 1:2], AF.Exp, [], [ssb], scale=-0.5)

    dma("sp", cf[:], cf_d[:, :], [], [cf], cf.slot)
    dma("sp", cb[:], cb_d[:, :], [], [cb], cb.slot)
    for r in range(8):
        for c0 in range(0, 9216, 2048):
            c1 = min(9216, c0 + 2048)
            dma("pool", winb[r * 128:(r + 1) * 128, c0:c1], w_in[r * 128:(r + 1) * 128, c0:c1], [], [Wd], wsl)
        dma("pool", woutb[r * 128:(r + 1) * 128, :], w_out[r * 128:(r + 1) * 128, :], [], [Wd], wsl)
        for c0 in range(0, 4096, 2048):
            dma("pool", wupb[r * 128:(r + 1) * 128, c0:c0 + 2048], w_up[r * 128:(r + 1) * 128, c0:c0 + 2048], [], [Wd], wsl)
    for r in range(32):
        dma("pool", wdnb[r * 128:(r + 1) * 128, :], w_dn[r * 128:(r + 1) * 128, :], [], [Wd], wsl)
    P.add("dve", lambda e: e.tensor_scalar(out=cf[:, C_QW:C_QW + 128], in0=cf[:, C_QW:C_QW + 128],
                                           scalar1=float(128 ** -0.5), scalar2=None, op0=ALU.mult),
          writes=[cf])

    winb_v = winb.rearrange("(kc p) n -> p kc n", p=128)
    woutb_v = woutb.rearrange("(kc p) n -> p kc n", p=128)
    worder = []
    wst = {"i": 0, "issued": 0}

    def wissue(k):
        kind, h = worder[k]
        buf = WBs[k % 2]
        if kind == "B":
            for j, c0 in enumerate([h * 256, 1024 + h * 256, 2048 + h * 256, 3072 + h * 256, 7168 + h * 256]):
                dma("sp", buf[:, :, j * 256:(j + 1) * 256], winb_v[:, :, c0:c0 + 256], [Wd], [buf], buf.slot)
        elif kind == "C":
            for j, c0 in enumerate([4096, 5120, 6144, 8192]):
                dma("sp", buf[:, :, j * 128:(j + 1) * 128], winb_v[:, :, c0 + h * 128:c0 + (h + 1) * 128], [Wd], [buf], buf.slot)
        else:
            dma("sp", buf[:, :, 0:D], woutb_v, [Wd], [buf], buf.slot)

    def wget(kind, h):
        k = wst["i"]
        assert worder[k] == (kind, h), (worder[k], kind, h)
        for kk in (k, k + 1):
            if kk < len(worder) and wst["issued"] <= kk:
                wissue(kk)
                wst["issued"] = kk + 1
        wst["i"] += 1
        return WBs[k % 2]
    wupb_v = wupb.rearrange("(kc p) n -> p kc n", p=128)

    def segment(kind, b, t0, ntile, nv, past, pastK, pastV, state, last, pos0):
        xsrc = xp[b] if kind == "p" else xs[b]
        ydst = yp[b] if kind == "p" else ys[b]
        kdst = kp[b] if kind == "p" else ks[b]
        vdst = vp[b] if kind == "p" else vs[b]
        kD = kpD[b] if kind == "p" else Buf(None)
        vD = vpD[b] if kind == "p" else Buf(None)
        ngrp = (ntile + 3) // 4
        gcols = [min(4, ntile - 4 * g) * 128 for g in range(ngrp)]
        dofs = C_DP if kind == "p" else C_DS
        kdofs = C_KDP if kind == "p" else C_KDS
        qdofs = C_QDP if kind == "p" else C_QDS
        gam = gamP if kind == "p" else gamS

        def load_x(ti, dst):
            if kind == "p":
                return dma("sp", dst[:, :], xsrc[t0 + ti * 128:t0 + (ti + 1) * 128, :], [], [dst], dst.slot)
            memset("dve", dst[:], 0.0, [dst])
            return dma("sp", dst[0:nv, :], xsrc[0:nv, :], [], [dst], dst.slot)

        def norm_T(src, nw_col, dstT, c0):
            act(junk[:], src[:], AF.Square, [src], [junk, ssA], accum_out=ssA[:, 0:1])
            rstd_ops(ssA, D)
            stt("dve", hb[:], src[:], ssA[:, 2:3], cf[:, nw_col:nw_col + D], ALU.mult, ALU.mult, [src, ssA, cf], [hb])
            for kc in range(8):
                tr(PST[:, kc * 128:(kc + 1) * 128], hb[:, kc * 128:(kc + 1) * 128], 128, [hb], [PST])
            act(dstT[:, :, c0:c0 + 128], PST[:].rearrange("p (k t) -> p k t", k=8), AF.Copy, [PST], [dstT])

        for ti in range(ntile):
            xt = xb[ti % 2]
            load_x(ti, xt)
            norm_T(xt, C_NMIX, hT[ti // 4], (ti % 4) * 128)

        pend_b = []
        for h in range(4):
            wr = wget("B", h)
            if state == "zero":
                memset("dve", S32[h][:], 0.0, [S32[h]])
                memset("dve", Sbf[h][:], 0.0, [Sbf[h]])
            elif state == "load":
                dma("sp", S32[h][:], st_in[b, h].rearrange("(c p) e -> p c e", p=128), [], [S32[h]], S32[h].slot)
                act(Sbf[h][:], S32[h][:], AF.Copy, [S32[h]], [Sbf[h]])
            for g in range(ngrp):
                nc_ = gcols[g]
                pc = pos0 + g * 512
                dma("sp", csb[:, :, 0:nc_], cs_d[:, :, pc:pc + nc_], [], [csb], csb.slot)
                for (wofs, dstT) in ((0, qTr), (256, kTr)):
                    for hf in range(2):
                        for kc in range(8):
                            mm(PS[hf][:, 0:nc_], wr[:, kc, wofs + hf * 128:wofs + (hf + 1) * 128], hT[g][:, kc, 0:nc_],
                               kc == 0, kc == 7, [wr, hT[g]], [PS[hf]])
                    tt("dve", rt[0][:, 0:nc_], PS[0][:, 0:nc_], csb[:, 0, 0:nc_], ALU.mult, [PS[0], csb], [rt[0]])
                    tt("dve", rt[1][:, 0:nc_], PS[1][:, 0:nc_], csb[:, 1, 0:nc_], ALU.mult, [PS[1], csb], [rt[1]])
                    tt("pool", dstT[:, 0, 0:nc_], rt[0][:, 0:nc_], rt[1][:, 0:nc_], ALU.subtract, [rt[0], rt[1]], [dstT])
                    tt("dve", rt[2][:, 0:nc_], PS[0][:, 0:nc_], csb[:, 1, 0:nc_], ALU.mult, [PS[0], csb], [rt[2]])
                    tt("dve", rt[3][:, 0:nc_], PS[1][:, 0:nc_], csb[:, 0, 0:nc_], ALU.mult, [PS[1], csb], [rt[3]])
                    tt("pool", dstT[:, 1, 0:nc_], rt[2][:, 0:nc_], rt[3][:, 0:nc_], ALU.add, [rt[2], rt[3]], [dstT])
                for t4 in range(nc_ // 128):
                    c0 = t4 * 128
                    for hf in range(2):
                        mm(PS[4][:, 0:128], kTr[:, hf, c0:c0 + 128], qTr[:, hf, c0:c0 + 128], hf == 0, hf == 1,
                           [kTr, qTr], [PS[4]])
                    tt("dve", scd[:], PS[4][:, 0:128], cf[:, dofs + h * 128:dofs + (h + 1) * 128], ALU.mult,
                       [PS[4], cf], [scd])
                    tr(PST[:, 0:128], kTr[:, 0, c0:c0 + 128], 128, [kTr], [PST])
                    tr(PST[:, 128:256], kTr[:, 1, c0:c0 + 128], 128, [kTr], [PST])
                    P.add("dve", (lambda hh: lambda e: e.tensor_scalar(
                        out=k2[:], in0=PST[:, 0:256], scalar1=cf[:, kdofs + hh:kdofs + hh + 1], scalar2=None,
                        op0=ALU.mult))(h), reads=[PST, cf], writes=[k2])
                    for kc in range(8):
                        mm(PS[2][:, 0:512], hT[g][:, kc, c0:c0 + 128], wr[:, kc, 512:1024], kc == 0, kc == 7,
                           [wr, hT[g]], [PS[2]])
                    for kc in range(8):
                        mm(PS[3][:, 0:256], hT[g][:, kc, c0:c0 + 128], wr[:, kc, 1024:1280], kc == 0, kc == 7,
                           [wr, hT[g]], [PS[3]])
                    act(vbf[:], PS[2][:, 0:256], AF.Copy, [PS[2]], [vbf])
                    act(gsi[:], PS[2][:, 256:512], AF.Silu, [PS[2]], [gsi])
                    act(sga[:], PS[3][:, 0:256], AF.Sigmoid, [PS[3]], [sga])
                    tt("pool", gate[:], gsi[:], sga[:], ALU.mult, [gsi, sga], [gate])
                    tt("pool", gate[:], gate[:], cf[:, C_RETW + h * 256:C_RETW + (h + 1) * 256], ALU.mult, [cf], [gate])
                    mm(PS[5][:, 0:256], scd[:], vbf[:], True, False, [scd, vbf], [PS[5]])
                    mm(PS[5][:, 0:256], qTr[:, 0, c0:c0 + 128], Sbf[h][:, 0, :], False, False, [qTr, Sbf[h]], [PS[5]])
                    mm(PS[5][:, 0:256], qTr[:, 1, c0:c0 + 128], Sbf[h][:, 1, :], False, True, [qTr, Sbf[h]], [PS[5]])
                    mm(PS[6][:, 0:256], k2[:, 0:128], vbf[:], True, True, [k2, vbf], [PS[6]])
                    mm(PS[6][:, 256:512], k2[:, 128:256], vbf[:], True, True, [k2, vbf], [PS[6]])
                    while pend_b:
                        pend_b.pop(0)()
                    stt("dve", S32[h][:].rearrange("p c e -> p (c e)"), S32[h][:].rearrange("p c e -> p (c e)"),
                        gam[h], PS[6][:, 0:512], ALU.mult, ALU.add, [PS[6]], [S32[h]])
                    act(Sbf[h][:], S32[h][:], AF.Copy, [S32[h]], [Sbf[h]])
                    act(junk[:, 0:256], PS[5][:, 0:256], AF.Square, [PS[5], cf], [junk, ssB],
                        scale=cf[:, qdofs + h:qdofs + h + 1], accum_out=ssB[:, 0:1])
                    rstd_ops(ssB, 256)
                    tt("dve", ssB[:, 3:4], ssB[:, 2:3], cf[:, qdofs + h:qdofs + h + 1], ALU.mult, [cf], [ssB])
                    rr = rr2[t4 % 2]
                    stt("dve", rr[:], PS[5][:, 0:256], ssB[:, 3:4], gate[:], ALU.mult, ALU.mult, [PS[5], ssB, gate], [rr])

                    def tail(rr=rr, g=g, h=h, c0=c0):
                        tr(PST[:, 256:384], rr[:, 0:128], 128, [rr], [PST])
                        tr(PST[:, 384:512], rr[:, 128:256], 128, [rr], [PST])
                        act(mixT[g][:, 2 * h:2 * h + 2, c0:c0 + 128], PST[:, 256:512].rearrange("p (k t) -> p k t", k=2),
                            AF.Copy, [PST], [mixT[g]])
                    pend_b.append(tail)
                while pend_b:
                    pend_b.pop(0)()
            if last:
                so = (sp_o if kind == "p" else ss_o)[b, h].rearrange("(c p) e -> p c e", p=128)
                dma("act", so, S32[h][:], [S32[h]], [], S32[h].slot)

        npb = past // 128
        for h in range(8):
            w4h = wget("C", h)
            for j in range(npb // 4):
                ksj, vsj = kst[j % 2], vst[j % 2]
                dma("sp", ksj[:], pastK[j * 512:(j + 1) * 512, h * 128:(h + 1) * 128].rearrange("(b p) d -> p b d", p=128),
                    [kD], [ksj], ksj.slot)
                dma("sp", vsj[:], pastV[j * 512:(j + 1) * 512, h * 128:(h + 1) * 128].rearrange("(b p) d -> p b d", p=128),
                    [vD], [vsj], vsj.slot)
                act(kstb[:], ksj[:], AF.Copy, [ksj], [kstb])
                P.add("pool", (lambda jj, vv: lambda e: e.tensor_copy(out=vh[:, jj * 4:(jj + 1) * 4, :], in_=vv[:]))(j, vsj),
                      reads=[vsj], writes=[vh])
                for bb in range(4):
                    tr(PST[:, bb * 128:(bb + 1) * 128], kstb[:, bb, :], 128, [kstb], [PST])
                P.add("dve", (lambda jj: lambda e: e.tensor_copy(out=kTh[:, jj * 512:(jj + 1) * 512], in_=PST[:, 0:512]))(j),
                      reads=[PST], writes=[kTh])
            def c1_mm(ti):
                g, c0 = ti // 4, (ti % 4) * 128
                pb = PS[0] if ti % 2 == 0 else PS[2]
                for kc in range(8):
                    mm(pb[:, 0:384], hT[g][:, kc, c0:c0 + 128], w4h[:, kc, 0:384], kc == 0, kc == 7, [w4h, hT[g]], [pb])

            def c1_chain(ti):
                rows = 128 if kind == "p" else nv
                pb = PS[0] if ti % 2 == 0 else PS[2]
                p2 = ti % 2
                ko, vo, sc_, qn_, knb_ = kout[p2], vout[p2], ssC2[p2], qn2[p2], knb2[p2]
                act(junk[:, 0:128], pb[:, 0:128], AF.Square, [pb], [junk, sc_], accum_out=sc_[:, 0:1])
                act(junk[:, 128:256], pb[:, 128:256], AF.Square, [pb], [junk, sc_], accum_out=sc_[:, 1:2])
                act(sc_[:, 0:2], sc_[:, 0:2], AF.Ln, [], [sc_], scale=1.0 / 128, bias=EPS)
                act(sc_[:, 2:4], sc_[:, 0:2], AF.Exp, [], [sc_], scale=-0.5)
                stt("dve", qn_[:], pb[:, 0:128], sc_[:, 2:3], cf[:, C_QW:C_QW + 128], ALU.mult, ALU.mult, [pb, sc_, cf], [qn_])
                stt("dve", ko[:], pb[:, 128:256], sc_[:, 3:4], cf[:, C_KW:C_KW + 128], ALU.mult, ALU.mult, [pb, sc_, cf], [ko])
                act(knb_[:], ko[:], AF.Copy, [ko], [knb_])
                act(vo[:], pb[:, 256:384], AF.Copy, [pb], [vo])
                P.add("pool", (lambda tb, vv: lambda e: e.tensor_copy(out=vh[:, tb, :], in_=vv[:]))(npb + ti, vo),
                      reads=[vo], writes=[vh])
                r0 = t0 + ti * 128
                dma("act", kdst[r0:r0 + rows, h * 128:(h + 1) * 128], ko[0:rows, :], [ko], [kD], ko.slot)
                dma("act", vdst[r0:r0 + rows, h * 128:(h + 1) * 128], vo[0:rows, :], [vo], [vD], vo.slot)

            def c1_tr(ti):
                p2 = ti % 2
                qn_, knb_ = qn2[p2], knb2[p2]
                tr(PST[:, 512:640], qn_[:], 128, [qn_], [PST])
                tr(PST[:, 640:768], knb_[:], 128, [knb_], [PST])
                P.add("dve", (lambda cc: lambda e: e.tensor_copy(out=qTh[:, cc:cc + 128], in_=PST[:, 512:640]))(ti * 128),
                      reads=[PST], writes=[qTh])
                P.add("dve", (lambda cc: lambda e: e.tensor_copy(out=kTh[:, cc:cc + 128], in_=PST[:, 640:768]))(past + ti * 128),
                      reads=[PST], writes=[kTh])

            c1_mm(0)
            for ti in range(ntile):
                if ti + 1 < ntile:
                    c1_mm(ti + 1)
                c1_chain(ti)
                c1_tr(ti)
            for g in range(ngrp):
                nc_ = gcols[g]
                for kc in range(8):
                    mm(PS[1][:, 0:nc_], w4h[:, kc, 384:512], hT[g][:, kc, 0:nc_], kc == 0, kc == 7, [w4h, hT[g]], [PS[1]])
                act(sgbT[:, g * 512:g * 512 + nc_], PS[1][:, 0:nc_], AF.Sigmoid, [PS[1]], [sgbT])
            its = []
            for g in range(ngrp):
                nq = gcols[g] if kind == "p" else nv
                nqb = (nq + 127) // 128
                kb_hi = npb + g * 4 + nqb - 1
                first = True
                for kb in range(kb_hi, -1, -1):
                    if kb >= npb + g * 4:
                        r = kb - (npb + g * 4)
                        col_lo, diag = r * 128, True
                    else:
                        col_lo, diag = 0, False
                    n = nq - col_lo
                    ksz = 128 if (kind == "p" or kb < npb) else nv
                    its.append(dict(g=g, kb=kb, col_lo=col_lo, diag=diag, n=n, ks=ksz, nd=min(128, n), q0=g * 512 + col_lo,
                                    first=first, last=(kb == 0), nq=nq))
                    first = False
            NI = len(its)
            Zp, Xp, Yp, Op = [PS[0], PS[1]], [PS[2], PS[3]], PS[4], [PS[5], PS[6]]

            def zgroup(dst, it, final_stop):
                ks_, n, nd = it["ks"], it["n"], it["nd"]
                kap = kTh[:, it["kb"] * 128:it["kb"] * 128 + ks_]
                qap = qTh[:, it["q0"]:it["q0"] + n]
                mm(dst[0:ks_, 0:n], kap, qap, True, final_stop and not it["diag"], [kTh, qTh], [dst])
                if it["diag"]:
                    mm(dst[0:ks_, 0:nd], cb[0:ks_, B_ID:B_ID + ks_], cb[0:ks_, B_MNEG:B_MNEG + nd], False, final_stop,
                       [cb], [dst])

            for s in range(NI + 2):
                if s < NI:
                    it = its[s]
                    zgroup(Zp[s % 2], it, True)
                if 0 <= s - 1 < NI:
                    i = s - 1
                    it = its[i]
                    ks_, n = it["ks"], it["n"]
                    zgroup(Xp[i % 2], it, False)
                    mm(Xp[i % 2][0:ks_, 0:n], cb[0:ks_, B_TRI:B_TRI + ks_], SPb[i % 2][0:ks_, 0:n], False, True,
                       [cb, SPb[i % 2]], [Xp[i % 2]])
                    mm(Yp[:, 0:n], cb[0:ks_, B_NEG:B_NEG + 128], SPb[i % 2][0:ks_, 0:n], True, True, [cb, SPb[i % 2]], [Yp])
                    if it["first"]:
                        memset("dve", Csum[:], 0.0, [Csum])
                    cl = it["col_lo"]
                    tt("dve", ARG[i % 2][0:ks_, 0:n], Xp[i % 2][0:ks_, 0:n], Csum[0:ks_, cl:cl + n], ALU.add,
                       [Xp[i % 2], Csum], [ARG[i % 2]])
                    tt("dve", Csum[:, cl:cl + n], Csum[:, cl:cl + n], Yp[:, 0:n], ALU.add, [Yp], [Csum])
                if s < NI:
                    it = its[s]
                    ks_, n = it["ks"], it["n"]
                    act(Eb[s % 2][0:ks_, 0:n], Zp[s % 2][0:ks_, 0:n], AF.Exp, [Zp[s % 2]], [Eb[s % 2]])
                    act(SPb[s % 2][0:ks_, 0:n], Eb[s % 2][0:ks_, 0:n], AF.Ln, [Eb[s % 2]], [SPb[s % 2]], bias=1.0)
                if 0 <= s - 1 < NI:
                    i = s - 1
                    it = its[i]
                    ks_, n = it["ks"], it["n"]
                    act(Wb[i % 2][0:ks_, 0:n], ARG[i % 2][0:ks_, 0:n], AF.Exp, [ARG[i % 2]], [Wb[i % 2]])
                if 0 <= s - 2 < NI:
                    i = s - 2
                    it = its[i]
                    ks_, n, g, cl = it["ks"], it["n"], it["g"], it["col_lo"]
                    opb = Op[g % 2]
                    mm(opb[:, cl:cl + n], vh[0:ks_, it["kb"], :], Wb[i % 2][0:ks_, 0:n], it["first"], it["last"],
                       [vh, Wb[i % 2]], [opb], skip=True)
                    if it["last"]:
                        nq = it["nq"]
                        tt("dve", otmp[:, 0:nq], opb[:, 0:nq], sgbT[:, g * 512:g * 512 + nq], ALU.mult, [opb, sgbT], [otmp])
                        tt("pool", mixT[g][:, h, 0:nq], mixT[g][:, h, 0:nq], otmp[:, 0:nq], ALU.add, [otmp], [mixT[g]])

        wout = wget("D", 0)
        for g in range(ngrp):
            nc_ = gcols[g]
            nt = nc_ // 128
            def d_proj(t4):
                ti = g * 4 + t4
                c0 = t4 * 128
                xt = xb[ti % 2]
                load_x(ti, xt)
                for hf in range(2):
                    pb = PS[(t4 % 2) * 2 + hf]
                    for kc in range(8):
                        mm(pb[:, 0:512], mixT[g][:, kc, c0:c0 + 128], wout[:, kc, hf * 512:(hf + 1) * 512], kc == 0, kc == 7,
                           [mixT[g], wout], [pb])
                    tt("dve", y2[t4][:, hf * 512:(hf + 1) * 512], pb[:, 0:512], xt[:, hf * 512:(hf + 1) * 512], ALU.add,
                       [pb, xt], [y2[t4]])

            d_proj(0)
            for t4 in range(nt):
                if t4 + 1 < nt:
                    d_proj(t4 + 1)
                norm_T(y2[t4], C_NMLP, hT[g], t4 * 128)
            h2T = hT[g]
            def d_load(fcg):
                wu, wd = wup[fcg % 2], wdn[fcg % 2]
                dma("sp", wu[:], wupb_v[:, :, fcg * 256:(fcg + 1) * 256], [Wd], [wu], wu.slot)
                dma("sp", wd[:], wdnb[fcg * 256:(fcg + 1) * 256, :].rearrange("(j p) n -> p j n", p=128), [Wd], [wd], wd.slot)

            def d_up(fcg):
                wu = wup[fcg % 2]
                ut = uT[fcg % 2]
                for j in range(2):
                    U = PS[2 + j % 2]
                    sq = sqb[j % 2]
                    for kc in range(8):
                        mm(U[:, 0:nc_], wu[:, kc, j * 128:(j + 1) * 128], h2T[:, kc, 0:nc_], kc == 0, kc == 7, [wu, h2T], [U])
                    act(sq[:, 0:nc_], U[:, 0:nc_], AF.Square, [U], [sq])
                    stt("dve", ut[:, j, 0:nc_], U[:, 0:nc_], 0.0, sq[:, 0:nc_], ALU.is_gt, ALU.mult, [U, sq], [ut])

            def d_down(fcg):
                wd = wdn[fcg % 2]
                ut = uT[fcg % 2]
                cnt = 0
                for t4 in range(nt):
                    c0 = t4 * 128
                    for hf in range(2):
                        Y = PS[4 + cnt % 2]
                        cnt += 1
                        for j in range(2):
                            mm(Y[:, 0:512], ut[:, j, c0:c0 + 128], wd[:, j, hf * 512:(hf + 1) * 512], j == 0, j == 1, [ut, wd], [Y])
                        tt("dve", y2[t4][:, hf * 512:(hf + 1) * 512], y2[t4][:, hf * 512:(hf + 1) * 512], Y[:, 0:512], ALU.add,
                           [Y], [y2[t4]])

            d_load(0)
            d_load(1)
            d_up(0)
            for fcg in range(16):
                if fcg + 1 < 16:
                    d_up(fcg + 1)
                d_down(fcg)
                if fcg + 2 < 16:
                    d_load(fcg + 2)
            for t4 in range(nt):
                ti = g * 4 + t4
                rows = 128 if kind == "p" else nv
                r0 = t0 + ti * 128
                dma("act", ydst[r0:r0 + rows, :], y2[t4][0:rows, :], [y2[t4]], [], y2[t4].slot)

    cf_tmp, cb_tmp, cs_tmp, gamP, gamS = make_consts(T, PAST, LS)
    for _ in range(NP * NSEG + NS):
        worder.extend([("B", h) for h in range(4)] + [("C", h) for h in range(8)] + [("D", 0)])
    for b in range(NP):
        for sg in range(NSEG):
            t0 = sg * SEG
            segment("p", b, t0, SEG // 128, 128, t0, kp[b], vp[b], "zero" if sg == 0 else "carry", sg == NSEG - 1, t0)
    for b in range(NS):
        segment("s", b, 0, 1, LS, PAST, ck[b], cv[b], "load", True, T)
    P.add("sp", None, extra=[("d", s, s.n) for s in P.slots if s.n > 0])
    P.emit()
    return nc, stack


_CACHE = {}


def _get(NP, T, NS, PAST, LS, SEG):
    key = (NP, T, NS, PAST, LS, SEG)
    if key not in _CACHE:
        _CACHE[key] = build(*key)
    return _CACHE[key][0]


def run(inputs, ncores, SEG):
    x_prompt = np.asarray(inputs["x_prompt"], np.float32)
    x_sample = np.asarray(inputs["x_sample"], np.float32)
    B, T, _ = x_prompt.shape
    BS, LS, _ = x_sample.shape
    ck = np.asarray(inputs["cache_sb_k"], np.float32)[0]
    cv = np.asarray(inputs["cache_sb_v"], np.float32)[0]
    PAST = ck.shape[1]
    st = np.asarray(inputs["state_ret"], np.float32)[0]
    NP, NS = B // ncores, BS // ncores
    nc = _get(NP, T, NS, PAST, LS, SEG)
    cf, cb, cs, _, _ = make_consts(T, PAST, LS)
    bc = lambda v, n: np.broadcast_to(np.asarray(v, np.float32).reshape(1, -1), (128, n))
    cf[:, C_NMIX:C_NMIX + D] = bc(inputs["norm_mix_w"][0], D)
    cf[:, C_NMLP:C_NMLP + D] = bc(inputs["norm_mlp_w"][0], D)
    cf[:, C_RETW:C_RETW + D] = bc(inputs["ret_norm_w"][0], D)
    cf[:, C_QW:C_QW + 128] = bc(inputs["sb_q_norm_w"][0], 128)
    cf[:, C_KW:C_KW + 128] = bc(inputs["sb_k_norm_w"][0], 128)
    shared = dict(
        w_in=np.ascontiguousarray(inputs["w_in"][0], np.float32),
        w_out=np.ascontiguousarray(inputs["w_out"][0], np.float32),
        w_up=np.ascontiguousarray(inputs["w_up"][0], np.float32),
        w_dn=np.ascontiguousarray(inputs["w_down"][0], np.float32),
        cf=cf, cb=cb, cs=np.ascontiguousarray(cs),
    )
    in_maps = []
    for c in range(ncores):
        m = dict(shared)
        m["xp"] = np.ascontiguousarray(x_prompt[c * NP:(c + 1) * NP])
        m["xs"] = np.ascontiguousarray(x_sample[c * NS:(c + 1) * NS])
        m["ck"] = np.ascontiguousarray(ck[c * NS:(c + 1) * NS].reshape(NS, PAST, D))
        m["cv"] = np.ascontiguousarray(cv[c * NS:(c + 1) * NS].reshape(NS, PAST, D))
        m["st"] = np.ascontiguousarray(st[c * NS:(c + 1) * NS])
        in_maps.append(m)
    res = run_bass_kernel_spmd(nc, in_maps, core_ids=list(range(ncores)))
    R = res.results
    cat = lambda k: np.concatenate([np.asarray(r[k], np.float32) for r in R], axis=0)
    y_p = cat("yp")
    y_s = cat("ys")
    k_p = cat("kp").reshape(1, B, T, 8, 128)
    v_p = cat("vp").reshape(1, B, T, 8, 128)
    s_p = cat("sp")[None]
    k_s = cat("ks").reshape(1, BS, LS, 8, 128)
    v_s = cat("vs").reshape(1, BS, LS, 8, 128)
    s_s = cat("ss")[None]
    return (y_p, y_s, k_p, v_p, s_p, k_s, v_s, s_s)


def kernel(**inputs):
    return run(inputs, NCORES, 1024)
```

```python
import numpy as np
import ml_dtypes
from contextlib import ExitStack
import concourse.bass as bass
import concourse.mybir as mybir
from concourse.bass_utils import run_bass_kernel_spmd

F32 = mybir.dt.float32
BF16 = mybir.dt.bfloat16
AF = mybir.ActivationFunctionType
ALU = mybir.AluOpType
BF = ml_dtypes.bfloat16

D = 1024
EPS = 1e-6
NCORES = 8
ENGS = ["pe", "act", "dve", "pool", "sp"]
EPOCH = 20000

C_NMIX, C_NMLP, C_RETW, C_QW, C_KW = 0, 1024, 2048, 3072, 3200
C_DP, C_DS, C_KDP, C_QDP, C_KDS, C_QDS = 3328, 3840, 4352, 4356, 4360, 4364
NCF = 4368
B_ID, B_TRI, B_NEG, B_MNEG = 0, 128, 256, 384
NCB = 512


class Slot:
    def __init__(self, h):
        self.h = h
        self.n = 0


class Buf:
    def __init__(self, t, slot=None):
        self.t = t
        self.w = None
        self.r = {}
        self.slot = slot

    def __getitem__(self, k):
        return self.t[k]


class Plan:
    def __init__(self, nc, stack):
        self.nc = nc
        self.stack = stack
        self.ops = {e: [] for e in ENGS}
        self.slots = []

    def slot(self):
        h = self.stack.enter_context(self.nc.semaphore(f"dq{len(self.slots)}"))
        s = Slot(h)
        self.slots.append(s)
        return s

    def add(self, eng, fn, reads=(), writes=(), slot=None, extra=()):
        deps = list(extra)
        for b in reads:
            if b.w is not None:
                deps.append(b.w)
            if getattr(b, "psum", False):
                for key, tkr in b.r.items():
                    if key != eng:
                        deps.append(tkr)
        for b in writes:
            if b.w is not None:
                deps.append(b.w)
            deps.extend(b.r.values())
        idx = len(self.ops[eng])
        if slot is not None:
            slot.n += 16
            assert slot.n < 60000
            tk = ("d", slot, slot.n)
        else:
            tk = ("e", eng, idx)
        for b in reads:
            b.r[tk[1]] = tk
        for b in writes:
            b.w = tk
            b.r = {}
        self.ops[eng].append([fn, deps, slot, False])
        return tk

    def emit(self):
        nc = self.nc
        for e in ENGS:
            for op in self.ops[e]:
                for d in op[1]:
                    if d[0] == "e" and not (d[1] == e and e == "pe"):
                        self.ops[d[1]][d[2]][3] = True
        val = {}
        nsem = {}
        for e in ENGS:
            c = 0
            val[e] = []
            for op in self.ops[e]:
                if op[3]:
                    c += 1
                val[e].append(c)
            nsem[e] = max(1, (c + EPOCH - 1) // EPOCH)
        sems = {e: [self.stack.enter_context(nc.semaphore(f"s_{e}{k}")) for k in range(nsem[e])]
                for e in ENGS}
        ops = self.ops

        def run2(e, eng):
            waited = {}
            for i, op in enumerate(ops[e]):
                for d in op[1]:
                    if d[0] == "e":
                        if d[1] == e and e == "pe":
                            continue
                        c = val[d[1]][d[2]]
                        key = d[1]
                        if waited.get(key, 0) >= c:
                            continue
                        waited[key] = c
                        eng.wait_ge(sems[d[1]][(c - 1) // EPOCH], (c - 1) % EPOCH + 1)
                    else:
                        key = id(d[1])
                        if waited.get(key, 0) >= d[2]:
                            continue
                        waited[key] = d[2]
                        eng.wait_ge(d[1].h, d[2])
                if op[0] is None:
                    continue
                ins = op[0](eng)
                if op[2] is not None:
                    ins.then_inc(op[2].h, 16)
                elif op[3]:
                    c = val[e][i]
                    ins.then_inc(sems[e][(c - 1) // EPOCH], 1)

        with nc.Block() as block:
            @block.tensor
            def _(t):
                run2("pe", t)

            @block.scalar
            def _(t):
                run2("act", t)

            @block.vector
            def _(t):
                run2("dve", t)

            @block.gpsimd
            def _(t):
                run2("pool", t)

            @block.sync
            def _(t):
                run2("sp", t)


def make_consts(T, PAST, LS):
    cf = np.zeros((128, NCF), np.float32)
    lg = np.log1p(-np.exp2(-5.0 - np.arange(4, dtype=np.float64)))
    i = np.arange(128)
    for h in range(4):
        g = lg[h]
        Dm = np.zeros((128, 128))
        for jj in range(128):
            for ii in range(128):
                cj, ci = jj // 64, ii // 64
                if cj == ci:
                    Dm[jj, ii] = np.exp(g * abs(ii - jj))
                elif cj < ci:
                    Dm[jj, ii] = np.exp(g * (ii - jj))
        qd = np.exp(g * (i + 1.0))
        cf[:, C_DP + h * 128:C_DP + (h + 1) * 128] = Dm / qd[None, :] / 16.0
        cf[:, C_KDP + h] = np.exp(g * (127.0 - i)) / 16.0
        cf[:, C_QDP + h] = qd
        Ds = np.zeros((128, 128))
        Ds[:LS, :LS] = np.exp(g * np.abs(i[:LS, None] - i[None, :LS]))
        qds = np.ones(128)
        qds[:LS] = np.exp(g * (i[:LS] + 1.0))
        cf[:, C_DS + h * 128:C_DS + (h + 1) * 128] = Ds / qds[None, :] / 16.0
        kds = np.zeros(128)
        kds[:LS] = np.exp(g * (LS - 1.0 - i[:LS])) / 16.0
        cf[:, C_KDS + h] = kds
        cf[:, C_QDS + h] = qds
    gamP = [float(np.exp(lg[h] * 128.0)) for h in range(4)]
    gamS = [float(np.exp(lg[h] * LS)) for h in range(4)]
    cb = np.zeros((128, NCB), np.float32)
    cb[:, B_ID:B_ID + 128] = np.eye(128)
    cb[:, B_TRI:B_TRI + 128] = -(i[:, None] >= i[None, :]).astype(np.float32)
    cb[:, B_NEG:B_NEG + 128] = -1.0
    cb[:, B_MNEG:B_MNEG + 128] = np.where(i[:, None] < i[None, :], 0.0, -30000.0)
    half = 128
    inv_freq = (10000.0 ** (-np.arange(half, dtype=np.float32) / half)).astype(np.float32)
    pos = np.concatenate([np.arange(T), PAST + np.arange(128)]).astype(np.float32)
    ang = (pos[None, :] * inv_freq[:, None]).astype(np.float32)
    cs = np.stack([np.cos(ang), np.sin(ang)], axis=1).astype(np.float32)
    return cf, cb.astype(BF), cs, gamP, gamS


def build(NP, T, NS, PAST, LS, SEG):
    nc = bass.Bass("TRN2", target_bir_lowering=False)
    stack = ExitStack()
    P = Plan(nc, stack)
    NG = SEG // 512
    NSEG = T // SEG
    TT = T + 128

    def dram(name, shape, dtype=F32, kind="ExternalInput"):
        return nc.dram_tensor(name, shape, dtype, kind=kind).ap()

    xp = dram("xp", [NP, T, D])
    xs = dram("xs", [NS, LS, D])
    ck = dram("ck", [NS, PAST, D])
    cv = dram("cv", [NS, PAST, D])
    st_in = dram("st", [NS, 4, 256, 256])
    w_in = dram("w_in", [D, 9216])
    w_out = dram("w_out", [D, D])
    w_up = dram("w_up", [D, 4096])
    w_dn = dram("w_dn", [4096, D])
    cf_d = dram("cf", [128, NCF])
    cb_d = dram("cb", [128, NCB], BF16)
    cs_d = dram("cs", [128, 2, TT])
    yp = dram("yp", [NP, T, D], kind="ExternalOutput")
    ys = dram("ys", [NS, LS, D], kind="ExternalOutput")
    kp = dram("kp", [NP, T, D], kind="ExternalOutput")
    vp = dram("vp", [NP, T, D], kind="ExternalOutput")
    sp_o = dram("sp", [NP, 4, 256, 256], kind="ExternalOutput")
    ks = dram("ks", [NS, LS, D], kind="ExternalOutput")
    vs = dram("vs", [NS, LS, D], kind="ExternalOutput")
    ss_o = dram("ss", [NS, 4, 256, 256], kind="ExternalOutput")
    winb = nc.dram_tensor("winb", [D, 9216], BF16).ap()
    woutb = nc.dram_tensor("woutb", [D, D], BF16).ap()
    wupb = nc.dram_tensor("wupb", [D, 4096], BF16).ap()
    wdnb = nc.dram_tensor("wdnb", [4096, D], BF16).ap()

    def SB(name, shape, dtype, slot=False):
        t = stack.enter_context(nc.sbuf_tensor("sb_" + name, shape, dtype))
        return Buf(t, P.slot() if slot else None)

    def PSB(name, shape, dtype):
        bb = Buf(stack.enter_context(nc.psum_tensor("pp_" + name, shape, dtype)))
        bb.psum = True
        return bb

    cf = SB("cf", [128, NCF], F32, True)
    cb = SB("cb", [128, NCB], BF16, True)
    hT = [SB(f"hT{g}", [128, 8, 512], BF16) for g in range(NG)]
    mixT = [SB(f"mx{g}", [128, 8, 512], BF16) for g in range(NG)]
    S32 = [SB(f"S{h}", [128, 2, 256], F32, True) for h in range(4)]
    Sbf = [SB(f"Sb{h}", [128, 2, 256], BF16) for h in range(4)]
    xb = [SB(f"xb{i}", [128, D], F32, True) for i in range(2)]
    hb = SB("hb", [128, D], BF16)
    junk = hb
    ssA = SB("ssA", [128, 4], F32)
    WBs = [SB(f"WB{i}", [128, 8, 1280], BF16, True) for i in range(2)]
    csb = SB("csb", [128, 2, 512], F32, True)
    qTr = SB("qTr", [128, 2, 512], BF16)
    kTr = SB("kTr", [128, 2, 512], BF16)
    k2 = SB("k2", [128, 256], BF16)
    vbf = SB("vbf", [128, 256], BF16)
    gsi = SB("gsi", [128, 256], F32)
    sga = SB("sga", [128, 256], F32)
    gate = SB("gate", [128, 256], F32)
    scd = SB("scd", [128, 128], BF16)
    ssB = SB("ssB", [128, 4], F32)
    rr2 = [SB(f"rr{i}", [128, 256], BF16) for i in range(2)]
    KMAX = max(T, PAST + 128)
    qTh = SB("qTh", [128, SEG], BF16)
    kTh = SB("kTh", [128, KMAX], BF16)
    vh = SB("vh", [128, KMAX // 128, 128], BF16)
    sgbT = SB("sgbT", [128, SEG], BF16)
    kst = [SB(f"kst{i}", [128, 4, 128], F32, True) for i in range(1)] * 2
    vst = [SB(f"vst{i}", [128, 4, 128], F32, True) for i in range(1)] * 2
    kstb = SB("kstb", [128, 4, 128], BF16)
    qn2 = [SB(f"qn{i}", [128, 128], BF16) for i in range(2)]
    knb2 = [SB(f"knb{i}", [128, 128], BF16) for i in range(2)]
    ssC2 = [SB(f"ssC{i}", [128, 4], F32) for i in range(2)]
    kout = [SB(f"kout{i}", [128, 128], F32, True) for i in range(2)]
    vout = [SB(f"vout{i}", [128, 128], F32, True) for i in range(2)]
    ssC = SB("ssC", [128, 4], F32)
    Eb = [SB(f"E{i}", [128, 512], F32) for i in range(2)]
    SPb = [SB(f"SP{i}", [128, 512], BF16) for i in range(2)]
    ARG = [SB(f"ARG{i}", [128, 512], F32) for i in range(2)]
    Wb = [SB(f"W{i}", [128, 512], BF16) for i in range(2)]
    Csum = SB("Csum", [128, 512], F32)
    otmp = SB("otmp", [128, 512], F32)
    rt = [Eb[0], Eb[1], ARG[0], ARG[1]]
    y2 = [SB(f"y2{i}", [128, D], F32, True) for i in range(4)]
    wup = [SB(f"wup{i}", [128, 8, 256], BF16, True) for i in range(2)]
    wdn = [SB(f"wdn{i}", [128, 2, D], BF16, True) for i in range(2)]
    uT = [SB(f"uT{i}", [128, 2, 512], BF16) for i in range(2)]
    sqb = [ARG[0], ARG[1]]
    PS = [PSB(f"ps{i}", [128, 512], F32) for i in range(7)]
    PST = PSB("pst", [128, 1024], BF16)
    wsl = P.slot()
    Wd = Buf(None, wsl)
    kpD = [Buf(None) for _ in range(NP)]
    vpD = [Buf(None) for _ in range(NP)]

    ident = lambda n=128: cb[0:n, B_ID:B_ID + n]

    def dma(q, out_ap, in_ap, reads, writes, slot):
        return P.add(q, lambda e: e.dma_start(out=out_ap, in_=in_ap), reads=reads, writes=writes, slot=slot)

    def mm(out_ap, lhsT, rhs, start, stop, reads, writes, skip=False):
        if skip:
            fn = lambda e: e.matmul(out_ap, lhsT=lhsT, rhs=rhs, start=start, stop=stop, skip_group_check=True)
        else:
            fn = lambda e: e.matmul(out_ap, lhsT=lhsT, rhs=rhs, start=start, stop=stop)
        return P.add("pe", fn, reads=reads, writes=writes)

    def tr(out_ap, in_ap, n, reads, writes):
        idn = ident(n)
        return P.add("pe", lambda e: e.transpose(out_ap, in_ap, idn), reads=list(reads) + [cb], writes=writes)

    def act(out_ap, in_ap, func, reads, writes, **kw):
        return P.add("act", lambda e: e.activation(out=out_ap, in_=in_ap, func=func, **kw), reads=reads, writes=writes)

    def stt(eng, out_ap, in0, scalar, in1, op0, op1, reads, writes):
        return P.add(eng, lambda e: e.scalar_tensor_tensor(out=out_ap, in0=in0, scalar=scalar, in1=in1, op0=op0, op1=op1),
                     reads=reads, writes=writes)

    def tt(eng, out_ap, in0, in1, op, reads, writes):
        return P.add(eng, lambda e: e.tensor_tensor(out=out_ap, in0=in0, in1=in1, op=op), reads=reads, writes=writes)

    def memset(eng, ap, v, writes):
        return P.add(eng, lambda e: e.memset(ap, v), writes=writes)

    def rstd_ops(ssb, n, reads_extra=()):
        act(ssb[:, 1:2], ssb[:, 0:1], AF.Ln, [], [ssb], scale=1.0 / n, bias=EPS)
        act(ssb[:, 2:3], ssb[:, 1:2], AF.Exp, [], [ssb], scale=-0.5)

    dma("sp", cf[:], cf_d[:, :], [], [cf], cf.slot)
    dma("sp", cb[:], cb_d[:, :], [], [cb], cb.slot)
    Wi = [Buf(None, P.slot()) for _ in range(2)]
    Wr = [Buf(None, P.slot()) for _ in range(2)]
    n = 0
    for r in range(8):
        for c0 in range(0, 9216, 2048):
            c1 = min(9216, c0 + 2048)
            dma("pool", winb[r * 128:(r + 1) * 128, c0:c1], w_in[r * 128:(r + 1) * 128, c0:c1], [], [Wi[n % 2]], Wi[n % 2].slot)
            n += 1
    for r in range(8):
        dma("pool", woutb[r * 128:(r + 1) * 128, :], w_out[r * 128:(r + 1) * 128, :], [], [Wr[n % 2]], Wr[n % 2].slot)
        n += 1
        for c0 in range(0, 4096, 2048):
            dma("pool", wupb[r * 128:(r + 1) * 128, c0:c0 + 2048], w_up[r * 128:(r + 1) * 128, c0:c0 + 2048], [], [Wr[n % 2]],
                Wr[n % 2].slot)
            n += 1
    for r in range(32):
        dma("pool", wdnb[r * 128:(r + 1) * 128, :], w_dn[r * 128:(r + 1) * 128, :], [], [Wr[n % 2]], Wr[n % 2].slot)
        n += 1
    P.add("dve", lambda e: e.tensor_scalar(out=cf[:, C_QW:C_QW + 128], in0=cf[:, C_QW:C_QW + 128],
                                           scalar1=float(128 ** -0.5), scalar2=None, op0=ALU.mult),
          writes=[cf])

    winb_v = winb.rearrange("(kc p) n -> p kc n", p=128)
    woutb_v = woutb.rearrange("(kc p) n -> p kc n", p=128)
    worder = []
    wst = {"i": 0, "issued": 0}

    def wissue(k):
        kind, h = worder[k]
        buf = WBs[k % 2]
        if kind == "B":
            for j, c0 in enumerate([h * 256, 1024 + h * 256, 2048 + h * 256, 3072 + h * 256, 7168 + h * 256]):
                dma("sp", buf[:, :, j * 256:(j + 1) * 256], winb_v[:, :, c0:c0 + 256], Wi, [buf], buf.slot)
        elif kind == "C":
            for j, c0 in enumerate([4096, 5120, 6144, 8192]):
                dma("sp", buf[:, :, j * 128:(j + 1) * 128], winb_v[:, :, c0 + h * 128:c0 + (h + 1) * 128], Wi, [buf], buf.slot)
        else:
            dma("sp", buf[:, :, 0:D], woutb_v, Wr, [buf], buf.slot)

    def wget(kind, h):
        k = wst["i"]
        assert worder[k] == (kind, h), (worder[k], kind, h)
        for kk in (k, k + 1):
            if kk < len(worder) and wst["issued"] <= kk:
                wissue(kk)
                wst["issued"] = kk + 1
        wst["i"] += 1
        return WBs[k % 2]
    wupb_v = wupb.rearrange("(kc p) n -> p kc n", p=128)

    def segment(kind, b, t0, ntile, nv, past, pastK, pastV, state, last, pos0):
        xsrc = xp[b] if kind == "p" else xs[b]
        ydst = yp[b] if kind == "p" else ys[b]
        kdst = kp[b] if kind == "p" else ks[b]
        vdst = vp[b] if kind == "p" else vs[b]
        kD = kpD[b] if kind == "p" else Buf(None)
        vD = vpD[b] if kind == "p" else Buf(None)
        ngrp = (ntile + 3) // 4
        gcols = [min(4, ntile - 4 * g) * 128 for g in range(ngrp)]
        dofs = C_DP if kind == "p" else C_DS
        kdofs = C_KDP if kind == "p" else C_KDS
        qdofs = C_QDP if kind == "p" else C_QDS
        gam = gamP if kind == "p" else gamS

        def load_x(ti, dst):
            if kind == "p":
                return dma("sp", dst[:, :], xsrc[t0 + ti * 128:t0 + (ti + 1) * 128, :], [], [dst], dst.slot)
            memset("dve", dst[:], 0.0, [dst])
            return dma("sp", dst[0:nv, :], xsrc[0:nv, :], [], [dst], dst.slot)

        def norm_T(src, nw_col, dstT, c0):
            act(junk[:], src[:], AF.Square, [src], [junk, ssA], accum_out=ssA[:, 0:1])
            rstd_ops(ssA, D)
            stt("dve", hb[:], src[:], ssA[:, 2:3], cf[:, nw_col:nw_col + D], ALU.mult, ALU.mult, [src, ssA, cf], [hb])
            for kc in range(8):
                tr(PST[:, kc * 128:(kc + 1) * 128], hb[:, kc * 128:(kc + 1) * 128], 128, [hb], [PST])
            act(dstT[:, :, c0:c0 + 128], PST[:].rearrange("p (k t) -> p k t", k=8), AF.Copy, [PST], [dstT])

        for ti in range(ntile):
            xt = xb[ti % 2]
            load_x(ti, xt)
            norm_T(xt, C_NMIX, hT[ti // 4], (ti % 4) * 128)

        pend_b = []
        for h in range(4):
            wr = wget("B", h)
            if state == "zero":
                memset("dve", S32[h][:], 0.0, [S32[h]])
                memset("dve", Sbf[h][:], 0.0, [Sbf[h]])
            elif state == "load":
                dma("sp", S32[h][:], st_in[b, h].rearrange("(c p) e -> p c e", p=128), [], [S32[h]], S32[h].slot)
                act(Sbf[h][:], S32[h][:], AF.Copy, [S32[h]], [Sbf[h]])
            for g in range(ngrp):
                nc_ = gcols[g]
                pc = pos0 + g * 512
                dma("sp", csb[:, :, 0:nc_], cs_d[:, :, pc:pc + nc_], [], [csb], csb.slot)
                for (wofs, dstT) in ((0, qTr), (256, kTr)):
                    for hf in range(2):
                        for kc in range(8):
                            mm(PS[hf][:, 0:nc_], wr[:, kc, wofs + hf * 128:wofs + (hf + 1) * 128], hT[g][:, kc, 0:nc_],
                               kc == 0, kc == 7, [wr, hT[g]], [PS[hf]])
                    tt("dve", rt[0][:, 0:nc_], PS[0][:, 0:nc_], csb[:, 0, 0:nc_], ALU.mult, [PS[0], csb], [rt[0]])
                    tt("dve", rt[1][:, 0:nc_], PS[1][:, 0:nc_], csb[:, 1, 0:nc_], ALU.mult, [PS[1], csb], [rt[1]])
                    tt("pool", dstT[:, 0, 0:nc_], rt[0][:, 0:nc_], rt[1][:, 0:nc_], ALU.subtract, [rt[0], rt[1]], [dstT])
                    tt("dve", rt[2][:, 0:nc_], PS[0][:, 0:nc_], csb[:, 1, 0:nc_], ALU.mult, [PS[0], csb], [rt[2]])
                    tt("dve", rt[3][:, 0:nc_], PS[1][:, 0:nc_], csb[:, 0, 0:nc_], ALU.mult, [PS[1], csb], [rt[3]])
                    tt("pool", dstT[:, 1, 0:nc_], rt[2][:, 0:nc_], rt[3][:, 0:nc_], ALU.add, [rt[2], rt[3]], [dstT])
                for t4 in range(nc_ // 128):
                    c0 = t4 * 128
                    for hf in range(2):
                        mm(PS[4][:, 0:128], kTr[:, hf, c0:c0 + 128], qTr[:, hf, c0:c0 + 128], hf == 0, hf == 1,
                           [kTr, qTr], [PS[4]])
                    tt("dve", scd[:], PS[4][:, 0:128], cf[:, dofs + h * 128:dofs + (h + 1) * 128], ALU.mult,
                       [PS[4], cf], [scd])
                    tr(PST[:, 0:128], kTr[:, 0, c0:c0 + 128], 128, [kTr], [PST])
                    tr(PST[:, 128:256], kTr[:, 1, c0:c0 + 128], 128, [kTr], [PST])
                    P.add("dve", (lambda hh: lambda e: e.tensor_scalar(
                        out=k2[:], in0=PST[:, 0:256], scalar1=cf[:, kdofs + hh:kdofs + hh + 1], scalar2=None,
                        op0=ALU.mult))(h), reads=[PST, cf], writes=[k2])
                    for kc in range(8):
                        mm(PS[2][:, 0:512], hT[g][:, kc, c0:c0 + 128], wr[:, kc, 512:1024], kc == 0, kc == 7,
                           [wr, hT[g]], [PS[2]])
                    for kc in range(8):
                        mm(PS[3][:, 0:256], hT[g][:, kc, c0:c0 + 128], wr[:, kc, 1024:1280], kc == 0, kc == 7,
                           [wr, hT[g]], [PS[3]])
                    act(vbf[:], PS[2][:, 0:256], AF.Copy, [PS[2]], [vbf])
                    act(gsi[:], PS[2][:, 256:512], AF.Silu, [PS[2]], [gsi])
                    act(sga[:], PS[3][:, 0:256], AF.Sigmoid, [PS[3]], [sga])
                    tt("pool", gate[:], gsi[:], sga[:], ALU.mult, [gsi, sga], [gate])
                    tt("pool", gate[:], gate[:], cf[:, C_RETW + h * 256:C_RETW + (h + 1) * 256], ALU.mult, [cf], [gate])
                    mm(PS[5][:, 0:256], scd[:], vbf[:], True, False, [scd, vbf], [PS[5]])
                    mm(PS[5][:, 0:256], qTr[:, 0, c0:c0 + 128], Sbf[h][:, 0, :], False, False, [qTr, Sbf[h]], [PS[5]])
                    mm(PS[5][:, 0:256], qTr[:, 1, c0:c0 + 128], Sbf[h][:, 1, :], False, True, [qTr, Sbf[h]], [PS[5]])
                    mm(PS[6][:, 0:256], k2[:, 0:128], vbf[:], True, True, [k2, vbf], [PS[6]])
                    mm(PS[6][:, 256:512], k2[:, 128:256], vbf[:], True, True, [k2, vbf], [PS[6]])
                    while pend_b:
                        pend_b.pop(0)()
                    stt("dve", S32[h][:].rearrange("p c e -> p (c e)"), S32[h][:].rearrange("p c e -> p (c e)"),
                        gam[h], PS[6][:, 0:512], ALU.mult, ALU.add, [PS[6]], [S32[h]])
                    act(Sbf[h][:], S32[h][:], AF.Copy, [S32[h]], [Sbf[h]])
                    act(junk[:, 0:256], PS[5][:, 0:256], AF.Square, [PS[5], cf], [junk, ssB],
                        scale=cf[:, qdofs + h:qdofs + h + 1], accum_out=ssB[:, 0:1])
                    rstd_ops(ssB, 256)
                    tt("dve", ssB[:, 3:4], ssB[:, 2:3], cf[:, qdofs + h:qdofs + h + 1], ALU.mult, [cf], [ssB])
                    rr = rr2[t4 % 2]
                    stt("dve", rr[:], PS[5][:, 0:256], ssB[:, 3:4], gate[:], ALU.mult, ALU.mult, [PS[5], ssB, gate], [rr])

                    def tail(rr=rr, g=g, h=h, c0=c0):
                        tr(PST[:, 256:384], rr[:, 0:128], 128, [rr], [PST])
                        tr(PST[:, 384:512], rr[:, 128:256], 128, [rr], [PST])
                        act(mixT[g][:, 2 * h:2 * h + 2, c0:c0 + 128], PST[:, 256:512].rearrange("p (k t) -> p k t", k=2),
                            AF.Copy, [PST], [mixT[g]])
                    pend_b.append(tail)
                while pend_b:
                    pend_b.pop(0)()
            if last:
                so = (sp_o if kind == "p" else ss_o)[b, h].rearrange("(c p) e -> p c e", p=128)
                dma("act", so, S32[h][:], [S32[h]], [], S32[h].slot)

        npb = past // 128
        for h in range(8):
            w4h = wget("C", h)
            for j in range(npb // 4):
                ksj, vsj = kst[j % 2], vst[j % 2]
                dma("sp", ksj[:], pastK[j * 512:(j + 1) * 512, h * 128:(h + 1) * 128].rearrange("(b p) d -> p b d", p=128),
                    [kD], [ksj], ksj.slot)
                dma("sp", vsj[:], pastV[j * 512:(j + 1) * 512, h * 128:(h + 1) * 128].rearrange("(b p) d -> p b d", p=128),
                    [vD], [vsj], vsj.slot)
                act(kstb[:], ksj[:], AF.Copy, [ksj], [kstb])
                P.add("pool", (lambda jj, vv: lambda e: e.tensor_copy(out=vh[:, jj * 4:(jj + 1) * 4, :], in_=vv[:]))(j, vsj),
                      reads=[vsj], writes=[vh])
                for bb in range(4):
                    tr(PST[:, bb * 128:(bb + 1) * 128], kstb[:, bb, :], 128, [kstb], [PST])
                P.add("dve", (lambda jj: lambda e: e.tensor_copy(out=kTh[:, jj * 512:(jj + 1) * 512], in_=PST[:, 0:512]))(j),
                      reads=[PST], writes=[kTh])
            def c1_mm(ti):
                g, c0 = ti // 4, (ti % 4) * 128
                pb = PS[0] if ti % 2 == 0 else PS[2]
                for kc in range(8):
                    mm(pb[:, 0:384], hT[g][:, kc, c0:c0 + 128], w4h[:, kc, 0:384], kc == 0, kc == 7, [w4h, hT[g]], [pb])

            def c1_chain(ti):
                rows = 128 if kind == "p" else nv
                pb = PS[0] if ti % 2 == 0 else PS[2]
                p2 = ti % 2
                ko, vo, sc_, qn_, knb_ = kout[p2], vout[p2], ssC2[p2], qn2[p2], knb2[p2]
                act(junk[:, 0:128], pb[:, 0:128], AF.Square, [pb], [junk, sc_], accum_out=sc_[:, 0:1])
                act(junk[:, 128:256], pb[:, 128:256], AF.Square, [pb], [junk, sc_], accum_out=sc_[:, 1:2])
                act(sc_[:, 0:2], sc_[:, 0:2], AF.Ln, [], [sc_], scale=1.0 / 128, bias=EPS)
                act(sc_[:, 2:4], sc_[:, 0:2], AF.Exp, [], [sc_], scale=-0.5)
                stt("dve", qn_[:], pb[:, 0:128], sc_[:, 2:3], cf[:, C_QW:C_QW + 128], ALU.mult, ALU.mult, [pb, sc_, cf], [qn_])
                stt("dve", ko[:], pb[:, 128:256], sc_[:, 3:4], cf[:, C_KW:C_KW + 128], ALU.mult, ALU.mult, [pb, sc_, cf], [ko])
                act(knb_[:], ko[:], AF.Copy, [ko], [knb_])
                act(vo[:], pb[:, 256:384], AF.Copy, [pb], [vo])
                P.add("pool", (lambda tb, vv: lambda e: e.tensor_copy(out=vh[:, tb, :], in_=vv[:]))(npb + ti, vo),
                      reads=[vo], writes=[vh])
                r0 = t0 + ti * 128
                dma("act", kdst[r0:r0 + rows, h * 128:(h + 1) * 128], ko[0:rows, :], [ko], [kD], ko.slot)
                dma("act", vdst[r0:r0 + rows, h * 128:(h + 1) * 128], vo[0:rows, :], [vo], [vD], vo.slot)

            def c1_tr(ti):
                p2 = ti % 2
                qn_, knb_ = qn2[p2], knb2[p2]
                tr(PST[:, 512:640], qn_[:], 128, [qn_], [PST])
                tr(PST[:, 640:768], knb_[:], 128, [knb_], [PST])
                P.add("dve", (lambda cc: lambda e: e.tensor_copy(out=qTh[:, cc:cc + 128], in_=PST[:, 512:640]))(ti * 128),
                      reads=[PST], writes=[qTh])
                P.add("dve", (lambda cc: lambda e: e.tensor_copy(out=kTh[:, cc:cc + 128], in_=PST[:, 640:768]))(past + ti * 128),
                      reads=[PST], writes=[kTh])

            c1_mm(0)
            for ti in range(ntile):
                if ti + 1 < ntile:
                    c1_mm(ti + 1)
                c1_chain(ti)
                c1_tr(ti)
            for g in range(ngrp):
                nc_ = gcols[g]
                for kc in range(8):
                    mm(PS[1][:, 0:nc_], w4h[:, kc, 384:512], hT[g][:, kc, 0:nc_], kc == 0, kc == 7, [w4h, hT[g]], [PS[1]])
                act(sgbT[:, g * 512:g * 512 + nc_], PS[1][:, 0:nc_], AF.Sigmoid, [PS[1]], [sgbT])
            its = []
            for g in range(ngrp):
                nq = gcols[g] if kind == "p" else nv
                nqb = (nq + 127) // 128
                kb_hi = npb + g * 4 + nqb - 1
                first = True
                for kb in range(kb_hi, -1, -1):
                    if kb >= npb + g * 4:
                        r = kb - (npb + g * 4)
                        col_lo, diag = r * 128, True
                    else:
                        col_lo, diag = 0, False
                    n = nq - col_lo
                    ksz = 128 if (kind == "p" or kb < npb) else nv
                    its.append(dict(g=g, kb=kb, col_lo=col_lo, diag=diag, n=n, ks=ksz, nd=min(128, n), q0=g * 512 + col_lo,
                                    first=first, last=(kb == 0), nq=nq))
                    first = False
            NI = len(its)
            Zp, Xp, Yp, Op = [PS[0], PS[1]], [PS[2], PS[3]], PS[4], [PS[5], PS[6]]

            def zgroup(dst, it, final_stop):
                ks_, n, nd = it["ks"], it["n"], it["nd"]
                kap = kTh[:, it["kb"] * 128:it["kb"] * 128 + ks_]
                qap = qTh[:, it["q0"]:it["q0"] + n]
                mm(dst[0:ks_, 0:n], kap, qap, True, final_stop and not it["diag"], [kTh, qTh], [dst])
                if it["diag"]:
                    mm(dst[0:ks_, 0:nd], cb[0:ks_, B_ID:B_ID + ks_], cb[0:ks_, B_MNEG:B_MNEG + nd], False, final_stop,
                       [cb], [dst])

            for s in range(NI + 2):
                if s < NI:
                    it = its[s]
                    zgroup(Zp[s % 2], it, True)
                if 0 <= s - 1 < NI:
                    i = s - 1
                    it = its[i]
                    ks_, n = it["ks"], it["n"]
                    zgroup(Xp[i % 2], it, False)
                    mm(Xp[i % 2][0:ks_, 0:n], cb[0:ks_, B_TRI:B_TRI + ks_], SPb[i % 2][0:ks_, 0:n], False, True,
                       [cb, SPb[i % 2]], [Xp[i % 2]])
                    mm(Yp[:, 0:n], cb[0:ks_, B_NEG:B_NEG + 128], SPb[i % 2][0:ks_, 0:n], True, True, [cb, SPb[i % 2]], [Yp])
                    if it["first"]:
                        memset("dve", Csum[:], 0.0, [Csum])
                    cl = it["col_lo"]
                    tt("dve", ARG[i % 2][0:ks_, 0:n], Xp[i % 2][0:ks_, 0:n], Csum[0:ks_, cl:cl + n], ALU.add,
                       [Xp[i % 2], Csum], [ARG[i % 2]])
                    tt("dve", Csum[:, cl:cl + n], Csum[:, cl:cl + n], Yp[:, 0:n], ALU.add, [Yp], [Csum])
                if s < NI:
                    it = its[s]
                    ks_, n = it["ks"], it["n"]
                    act(Eb[s % 2][0:ks_, 0:n], Zp[s % 2][0:ks_, 0:n], AF.Exp, [Zp[s % 2]], [Eb[s % 2]])
                    act(SPb[s % 2][0:ks_, 0:n], Eb[s % 2][0:ks_, 0:n], AF.Ln, [Eb[s % 2]], [SPb[s % 2]], bias=1.0)
                if 0 <= s - 1 < NI:
                    i = s - 1
                    it = its[i]
                    ks_, n = it["ks"], it["n"]
                    act(Wb[i % 2][0:ks_, 0:n], ARG[i % 2][0:ks_, 0:n], AF.Exp, [ARG[i % 2]], [Wb[i % 2]])
                if 0 <= s - 2 < NI:
                    i = s - 2
                    it = its[i]
                    ks_, n, g, cl = it["ks"], it["n"], it["g"], it["col_lo"]
                    opb = Op[g % 2]
                    mm(opb[:, cl:cl + n], vh[0:ks_, it["kb"], :], Wb[i % 2][0:ks_, 0:n], it["first"], it["last"],
                       [vh, Wb[i % 2]], [opb], skip=True)
                    if it["last"]:
                        nq = it["nq"]
                        tt("dve", otmp[:, 0:nq], opb[:, 0:nq], sgbT[:, g * 512:g * 512 + nq], ALU.mult, [opb, sgbT], [otmp])
                        tt("pool", mixT[g][:, h, 0:nq], mixT[g][:, h, 0:nq], otmp[:, 0:nq], ALU.add, [otmp], [mixT[g]])

        wout = wget("D", 0)
        for g in range(ngrp):
            nc_ = gcols[g]
            nt = nc_ // 128
            def d_proj(t4):
                ti = g * 4 + t4
                c0 = t4 * 128
                xt = xb[ti % 2]
                load_x(ti, xt)
                for hf in range(2):
                    pb = PS[(t4 % 2) * 2 + hf]
                    for kc in range(8):
                        mm(pb[:, 0:512], mixT[g][:, kc, c0:c0 + 128], wout[:, kc, hf * 512:(hf + 1) * 512], kc == 0, kc == 7,
                           [mixT[g], wout], [pb])
                    tt("dve", y2[t4][:, hf * 512:(hf + 1) * 512], pb[:, 0:512], xt[:, hf * 512:(hf + 1) * 512], ALU.add,
                       [pb, xt], [y2[t4]])

            d_proj(0)
            for t4 in range(nt):
                if t4 + 1 < nt:
                    d_proj(t4 + 1)
                norm_T(y2[t4], C_NMLP, hT[g], t4 * 128)
            h2T = hT[g]
            def d_load(fcg):
                wu, wd = wup[fcg % 2], wdn[fcg % 2]
                dma("sp", wu[:], wupb_v[:, :, fcg * 256:(fcg + 1) * 256], Wr, [wu], wu.slot)
                dma("sp", wd[:], wdnb[fcg * 256:(fcg + 1) * 256, :].rearrange("(j p) n -> p j n", p=128), Wr, [wd], wd.slot)

            def d_up(fcg):
                wu = wup[fcg % 2]
                ut = uT[fcg % 2]
                for j in range(2):
                    U = PS[2 + j % 2]
                    sq = sqb[j % 2]
                    for kc in range(8):
                        mm(U[:, 0:nc_], wu[:, kc, j * 128:(j + 1) * 128], h2T[:, kc, 0:nc_], kc == 0, kc == 7, [wu, h2T], [U])
                    act(sq[:, 0:nc_], U[:, 0:nc_], AF.Square, [U], [sq])
                    stt("dve", ut[:, j, 0:nc_], U[:, 0:nc_], 0.0, sq[:, 0:nc_], ALU.is_gt, ALU.mult, [U, sq], [ut])

            def d_down(fcg):
                wd = wdn[fcg % 2]
                ut = uT[fcg % 2]
                cnt = 0
                for t4 in range(nt):
                    c0 = t4 * 128
                    for hf in range(2):
                        Y = PS[4 + cnt % 2]
                        cnt += 1
                        for j in range(2):
                            mm(Y[:, 0:512], ut[:, j, c0:c0 + 128], wd[:, j, hf * 512:(hf + 1) * 512], j == 0, j == 1, [ut, wd], [Y])
                        tt("dve", y2[t4][:, hf * 512:(hf + 1) * 512], y2[t4][:, hf * 512:(hf + 1) * 512], Y[:, 0:512], ALU.add,
                           [Y], [y2[t4]])

            d_load(0)
            d_load(1)
            d_up(0)
            for fcg in range(16):
                if fcg + 1 < 16:
                    d_up(fcg + 1)
                d_down(fcg)
                if fcg + 2 < 16:
                    d_load(fcg + 2)
            for t4 in range(nt):
                ti = g * 4 + t4
                rows = 128 if kind == "p" else nv
                r0 = t0 + ti * 128
                dma("act", ydst[r0:r0 + rows, :], y2[t4][0:rows, :], [y2[t4]], [], y2[t4].slot)

    cf_tmp, cb_tmp, cs_tmp, gamP, gamS = make_consts(T, PAST, LS)
    for _ in range(NP * NSEG + NS):
        worder.extend([("B", h) for h in range(4)] + [("C", h) for h in range(8)] + [("D", 0)])
    for b in range(NP):
        for sg in range(NSEG):
            t0 = sg * SEG
            segment("p", b, t0, SEG // 128, 128, t0, kp[b], vp[b], "zero" if sg == 0 else "carry", sg == NSEG - 1, t0)
    for b in range(NS):
        segment("s", b, 0, 1, LS, PAST, ck[b], cv[b], "load", True, T)
    P.add("sp", None, extra=[("d", s, s.n) for s in P.slots if s.n > 0])
    P.emit()
    return nc, stack


_CACHE = {}


def _get(NP, T, NS, PAST, LS, SEG):
    key = (NP, T, NS, PAST, LS, SEG)
    if key not in _CACHE:
        _CACHE[key] = build(*key)
    return _CACHE[key][0]


def run(inputs, ncores, SEG):
    x_prompt = np.asarray(inputs["x_prompt"], np.float32)
    x_sample = np.asarray(inputs["x_sample"], np.float32)
    B, T, _ = x_prompt.shape
    BS, LS, _ = x_sample.shape
    ck = np.asarray(inputs["cache_sb_k"], np.float32)[0]
    cv = np.asarray(inputs["cache_sb_v"], np.float32)[0]
    PAST = ck.shape[1]
    st = np.asarray(inputs["state_ret"], np.float32)[0]
    NP, NS = B // ncores, BS // ncores
    nc = _get(NP, T, NS, PAST, LS, SEG)
    cf, cb, cs, _, _ = make_consts(T, PAST, LS)
    bc = lambda v, n: np.broadcast_to(np.asarray(v, np.float32).reshape(1, -1), (128, n))
    cf[:, C_NMIX:C_NMIX + D] = bc(inputs["norm_mix_w"][0], D)
    cf[:, C_NMLP:C_NMLP + D] = bc(inputs["norm_mlp_w"][0], D)
    cf[:, C_RETW:C_RETW + D] = bc(inputs["ret_norm_w"][0], D)
    cf[:, C_QW:C_QW + 128] = bc(inputs["sb_q_norm_w"][0], 128)
    cf[:, C_KW:C_KW + 128] = bc(inputs["sb_k_norm_w"][0], 128)
    shared = dict(
        w_in=np.ascontiguousarray(inputs["w_in"][0], np.float32),
        w_out=np.ascontiguousarray(inputs["w_out"][0], np.float32),
        w_up=np.ascontiguousarray(inputs["w_up"][0], np.float32),
        w_dn=np.ascontiguousarray(inputs["w_down"][0], np.float32),
        cf=cf, cb=cb, cs=np.ascontiguousarray(cs),
    )
    in_maps = []
    for c in range(ncores):
        m = dict(shared)
        m["xp"] = np.ascontiguousarray(x_prompt[c * NP:(c + 1) * NP])
        m["xs"] = np.ascontiguousarray(x_sample[c * NS:(c + 1) * NS])
        m["ck"] = np.ascontiguousarray(ck[c * NS:(c + 1) * NS].reshape(NS, PAST, D))
        m["cv"] = np.ascontiguousarray(cv[c * NS:(c + 1) * NS].reshape(NS, PAST, D))
        m["st"] = np.ascontiguousarray(st[c * NS:(c + 1) * NS])
        in_maps.append(m)
    res = run_bass_kernel_spmd(nc, in_maps, core_ids=list(range(ncores)))
    R = res.results
    cat = lambda k: np.concatenate([np.asarray(r[k], np.float32) for r in R], axis=0)
    y_p = cat("yp")
    y_s = cat("ys")
    k_p = cat("kp").reshape(1, B, T, 8, 128)
    v_p = cat("vp").reshape(1, B, T, 8, 128)
    s_p = cat("sp")[None]
    k_s = cat("ks").reshape(1, BS, LS, 8, 128)
    v_s = cat("vs").reshape(1, BS, LS, 8, 128)
    s_s = cat("ss")[None]
    return (y_p, y_s, k_p, v_p, s_p, k_s, v_s, s_s)


def kernel(**inputs):
    return run(inputs, NCORES, 1024)
```

```python
import numpy as np
import ml_dtypes
from contextlib import ExitStack
import concourse.bass as bass
import concourse.mybir as mybir
from concourse.bass_utils import run_bass_kernel_spmd

F32 = mybir.dt.float32
BF16 = mybir.dt.bfloat16
AF = mybir.ActivationFunctionType
ALU = mybir.AluOpType
BF = ml_dtypes.bfloat16

D = 1024
EPS = 1e-6
NCORES = 8
ENGS = ["pe", "act", "dve", "pool", "sp"]
EPOCH = 20000

C_NMIX, C_NMLP, C_RETW, C_QW, C_KW = 0, 1024, 2048, 3072, 3200
C_DP, C_DS, C_KDP, C_QDP, C_KDS, C_QDS = 3328, 3840, 4352, 4356, 4360, 4364
NCF = 4368
B_ID, B_TRI, B_NEG, B_MNEG = 0, 128, 256, 384
NCB = 512


class Slot:
    def __init__(self, h):
        self.h = h
        self.n = 0


class Buf:
    def __init__(self, t, slot=None):
        self.t = t
        self.w = None
        self.r = {}
        self.slot = slot

    def __getitem__(self, k):
        return self.t[k]


class Plan:
    def __init__(self, nc, stack):
        self.nc = nc
        self.stack = stack
        self.ops = {e: [] for e in ENGS}
        self.slots = []

    def slot(self):
        h = self.stack.enter_context(self.nc.semaphore(f"dq{len(self.slots)}"))
        s = Slot(h)
        self.slots.append(s)
        return s

    def add(self, eng, fn, reads=(), writes=(), slot=None, extra=()):
        deps = list(extra)
        for b in reads:
            if b.w is not None:
                deps.append(b.w)
            if getattr(b, "psum", False):
                for key, tkr in b.r.items():
                    if key != eng:
                        deps.append(tkr)
        for b in writes:
            if b.w is not None:
                deps.append(b.w)
            deps.extend(b.r.values())
        idx = len(self.ops[eng])
        if slot is not None:
            slot.n += 16
            assert slot.n < 60000
            tk = ("d", slot, slot.n)
        else:
            tk = ("e", eng, idx)
        for b in reads:
            b.r[tk[1]] = tk
        for b in writes:
            b.w = tk
            b.r = {}
        self.ops[eng].append([fn, deps, slot, False])
        return tk

    def emit(self):
        nc = self.nc
        for e in ENGS:
            for op in self.ops[e]:
                for d in op[1]:
                    if d[0] == "e" and not (d[1] == e and e == "pe"):
                        self.ops[d[1]][d[2]][3] = True
        val = {}
        nsem = {}
        for e in ENGS:
            c = 0
            val[e] = []
            for op in self.ops[e]:
                if op[3]:
                    c += 1
                val[e].append(c)
            nsem[e] = max(1, (c + EPOCH - 1) // EPOCH)
        sems = {e: [self.stack.enter_context(nc.semaphore(f"s_{e}{k}")) for k in range(nsem[e])]
                for e in ENGS}
        ops = self.ops

        def run2(e, eng):
            waited = {}
            for i, op in enumerate(ops[e]):
                for d in op[1]:
                    if d[0] == "e":
                        if d[1] == e and e == "pe":
                            continue
                        c = val[d[1]][d[2]]
                        key = d[1]
                        if waited.get(key, 0) >= c:
                            continue
                        waited[key] = c
                        eng.wait_ge(sems[d[1]][(c - 1) // EPOCH], (c - 1) % EPOCH + 1)
                    else:
                        key = id(d[1])
                        if waited.get(key, 0) >= d[2]:
                            continue
                        waited[key] = d[2]
                        eng.wait_ge(d[1].h, d[2])
                if op[0] is None:
                    continue
                ins = op[0](eng)
                if op[2] is not None:
                    ins.then_inc(op[2].h, 16)
                elif op[3]:
                    c = val[e][i]
                    ins.then_inc(sems[e][(c - 1) // EPOCH], 1)

        with nc.Block() as block:
            @block.tensor
            def _(t):
                run2("pe", t)

            @block.scalar
            def _(t):
                run2("act", t)

            @block.vector
            def _(t):
                run2("dve", t)

            @block.gpsimd
            def _(t):
                run2("pool", t)

            @block.sync
            def _(t):
                run2("sp", t)


def make_consts(T, PAST, LS):
    cf = np.zeros((128, NCF), np.float32)
    lg = np.log1p(-np.exp2(-5.0 - np.arange(4, dtype=np.float64)))
    i = np.arange(128)
    for h in range(4):
        g = lg[h]
        Dm = np.zeros((128, 128))
        for jj in range(128):
            for ii in range(128):
                cj, ci = jj // 64, ii // 64
                if cj == ci:
                    Dm[jj, ii] = np.exp(g * abs(ii - jj))
                elif cj < ci:
                    Dm[jj, ii] = np.exp(g * (ii - jj))
        qd = np.exp(g * (i + 1.0))
        cf[:, C_DP + h * 128:C_DP + (h + 1) * 128] = Dm / qd[None, :] / 16.0
        cf[:, C_KDP + h] = np.exp(g * (127.0 - i)) / 16.0
        cf[:, C_QDP + h] = qd
        Ds = np.zeros((128, 128))
        Ds[:LS, :LS] = np.exp(g * np.abs(i[:LS, None] - i[None, :LS]))
        qds = np.ones(128)
        qds[:LS] = np.exp(g * (i[:LS] + 1.0))
        cf[:, C_DS + h * 128:C_DS + (h + 1) * 128] = Ds / qds[None, :] / 16.0
        kds = np.zeros(128)
        kds[:LS] = np.exp(g * (LS - 1.0 - i[:LS])) / 16.0
        cf[:, C_KDS + h] = kds
        cf[:, C_QDS + h] = qds
    gamP = [float(np.exp(lg[h] * 128.0)) for h in range(4)]
    gamS = [float(np.exp(lg[h] * LS)) for h in range(4)]
    cb = np.zeros((128, NCB), np.float32)
    cb[:, B_ID:B_ID + 128] = np.eye(128)
    cb[:, B_TRI:B_TRI + 128] = -(i[:, None] >= i[None, :]).astype(np.float32)
    cb[:, B_NEG:B_NEG + 128] = -1.0
    cb[:, B_MNEG:B_MNEG + 128] = np.where(i[:, None] < i[None, :], 0.0, -30000.0)
    half = 128
    inv_freq = (10000.0 ** (-np.arange(half, dtype=np.float32) / half)).astype(np.float32)
    pos = np.concatenate([np.arange(T), PAST + np.arange(128)]).astype(np.float32)
    ang = (pos[None, :] * inv_freq[:, None]).astype(np.float32)
    cs = np.stack([np.cos(ang), np.sin(ang)], axis=1).astype(np.float32)
    return cf, cb.astype(BF), cs, gamP, gamS


def build(NP, T, NS, PAST, LS, SEG):
    nc = bass.Bass("TRN2", target_bir_lowering=False)
    stack = ExitStack()
    P = Plan(nc, stack)
    NG = SEG // 512
    NSEG = T // SEG
    TT = T + 128

    def dram(name, shape, dtype=F32, kind="ExternalInput"):
        return nc.dram_tensor(name, shape, dtype, kind=kind).ap()

    xp = dram("xp", [NP, T, D])
    xs = dram("xs", [NS, LS, D])
    ck = dram("ck", [NS, PAST, D])
    cv = dram("cv", [NS, PAST, D])
    st_in = dram("st", [NS, 4, 256, 256])
    w_in = dram("w_in", [D, 9216])
    w_out = dram("w_out", [D, D])
    w_up = dram("w_up", [D, 4096])
    w_dn = dram("w_dn", [4096, D])
    cf_d = dram("cf", [128, NCF])
    cb_d = dram("cb", [128, NCB], BF16)
    cs_d = dram("cs", [128, 2, TT])
    yp = dram("yp", [NP, T, D], kind="ExternalOutput")
    ys = dram("ys", [NS, LS, D], kind="ExternalOutput")
    kp = dram("kp", [NP, T, D], kind="ExternalOutput")
    vp = dram("vp", [NP, T, D], kind="ExternalOutput")
    sp_o = dram("sp", [NP, 4, 256, 256], kind="ExternalOutput")
    ks = dram("ks", [NS, LS, D], kind="ExternalOutput")
    vs = dram("vs", [NS, LS, D], kind="ExternalOutput")
    ss_o = dram("ss", [NS, 4, 256, 256], kind="ExternalOutput")
    winb = nc.dram_tensor("winb", [D, 9216], BF16).ap()
    woutb = nc.dram_tensor("woutb", [D, D], BF16).ap()
    wupb = nc.dram_tensor("wupb", [D, 4096], BF16).ap()
    wdnb = nc.dram_tensor("wdnb", [4096, D], BF16).ap()
    kTs = nc.dram_tensor("kTs", [NP, 8, 128, T], BF16).ap()
    vTs = nc.dram_tensor("vTs", [NP, 8, 128, T // 128, 128], BF16).ap()

    def SB(name, shape, dtype, slot=False):
        t = stack.enter_context(nc.sbuf_tensor("sb_" + name, shape, dtype))
        return Buf(t, P.slot() if slot else None)

    def PSB(name, shape, dtype):
        bb = Buf(stack.enter_context(nc.psum_tensor("pp_" + name, shape, dtype)))
        bb.psum = True
        return bb

    cf = SB("cf", [128, NCF], F32, True)
    cb = SB("cb", [128, NCB], BF16, True)
    hT = [SB(f"hT{g}", [128, 8, 512], BF16) for g in range(NG)]
    mixT = [SB(f"mx{g}", [128, 8, 512], BF16) for g in range(NG)]
    S32 = [SB(f"S{h}", [128, 2, 256], F32, True) for h in range(4)]
    Sbf = [SB(f"Sb{h}", [128, 2, 256], BF16) for h in range(4)]
    xb = [SB(f"xb{i}", [128, D], F32, True) for i in range(2)]
    hb = SB("hb", [128, D], BF16)
    junk = hb
    ssA = SB("ssA", [128, 4], F32)
    WBs = [SB(f"WB{i}", [128, 8, 1280], BF16, True) for i in range(2)]
    csb = SB("csb", [128, 2, 512], F32, True)
    qTr = SB("qTr", [128, 2, 512], BF16)
    kTr = SB("kTr", [128, 2, 512], BF16)
    k2 = SB("k2", [128, 256], BF16)
    vbf = SB("vbf", [128, 256], BF16)
    gsi = SB("gsi", [128, 256], F32)
    sga = SB("sga", [128, 256], F32)
    gate = SB("gate", [128, 256], F32)
    scd = SB("scd", [128, 128], BF16)
    ssB = SB("ssB", [128, 4], F32)
    rr2 = [SB(f"rr{i}", [128, 256], BF16) for i in range(2)]
    KMAX = max(T, PAST + 128)
    qTh = SB("qTh", [128, SEG], BF16)
    kTh = SB("kTh", [128, KMAX], BF16, True)
    vh = SB("vh", [128, KMAX // 128, 128], BF16, True)
    sgbT = SB("sgbT", [128, SEG], BF16)
    kst = [SB(f"kst{i}", [128, 4, 128], F32, True) for i in range(1)] * 2
    vst = [SB(f"vst{i}", [128, 4, 128], F32, True) for i in range(1)] * 2
    kstb = SB("kstb", [128, 4, 128], BF16)
    qn2 = [SB(f"qn{i}", [128, 128], BF16) for i in range(2)]
    knb2 = [SB(f"knb{i}", [128, 128], BF16) for i in range(2)]
    ssC2 = [SB(f"ssC{i}", [128, 4], F32) for i in range(2)]
    kout = [SB(f"kout{i}", [128, 128], F32, True) for i in range(2)]
    vout = [SB(f"vout{i}", [128, 128], F32, True) for i in range(2)]
    ssC = SB("ssC", [128, 4], F32)
    Eb = [SB(f"E{i}", [128, 512], F32) for i in range(2)]
    SPb = [SB(f"SP{i}", [128, 512], BF16) for i in range(2)]
    ARG = [SB(f"ARG{i}", [128, 512], F32) for i in range(2)]
    Wb = [SB(f"W{i}", [128, 512], BF16) for i in range(2)]
    Csum = SB("Csum", [128, 512], F32)
    otmp = SB("otmp", [128, 512], F32)
    rt = [Eb[0], Eb[1], ARG[0], ARG[1]]
    y2 = [SB(f"y2{i}", [128, D], F32, True) for i in range(4)]
    wup = [SB(f"wup{i}", [128, 8, 256], BF16, True) for i in range(2)]
    wdn = [SB(f"wdn{i}", [128, 2, D], BF16, True) for i in range(2)]
    uT = [SB(f"uT{i}", [128, 2, 512], BF16) for i in range(2)]
    sqb = [ARG[0], ARG[1]]
    PS = [PSB(f"ps{i}", [128, 512], F32) for i in range(7)]
    PST = PSB("pst", [128, 1024], BF16)
    wsl = P.slot()
    Wd = Buf(None, wsl)
    kpD = [Buf(None) for _ in range(NP)]
    vpD = [Buf(None) for _ in range(NP)]
    kTsD = [Buf(None) for _ in range(NP)]
    vTsD = [Buf(None) for _ in range(NP)]

    ident = lambda n=128: cb[0:n, B_ID:B_ID + n]

    def dma(q, out_ap, in_ap, reads, writes, slot):
        return P.add(q, lambda e: e.dma_start(out=out_ap, in_=in_ap), reads=reads, writes=writes, slot=slot)

    def mm(out_ap, lhsT, rhs, start, stop, reads, writes, skip=False):
        if skip:
            fn = lambda e: e.matmul(out_ap, lhsT=lhsT, rhs=rhs, start=start, stop=stop, skip_group_check=True)
        else:
            fn = lambda e: e.matmul(out_ap, lhsT=lhsT, rhs=rhs, start=start, stop=stop)
        return P.add("pe", fn, reads=reads, writes=writes)

    def tr(out_ap, in_ap, n, reads, writes):
        idn = ident(n)
        return P.add("pe", lambda e: e.transpose(out_ap, in_ap, idn), reads=list(reads) + [cb], writes=writes)

    def act(out_ap, in_ap, func, reads, writes, **kw):
        return P.add("act", lambda e: e.activation(out=out_ap, in_=in_ap, func=func, **kw), reads=reads, writes=writes)

    def stt(eng, out_ap, in0, scalar, in1, op0, op1, reads, writes):
        return P.add(eng, lambda e: e.scalar_tensor_tensor(out=out_ap, in0=in0, scalar=scalar, in1=in1, op0=op0, op1=op1),
                     reads=reads, writes=writes)

    def tt(eng, out_ap, in0, in1, op, reads, writes):
        return P.add(eng, lambda e: e.tensor_tensor(out=out_ap, in0=in0, in1=in1, op=op), reads=reads, writes=writes)

    def memset(eng, ap, v, writes):
        return P.add(eng, lambda e: e.memset(ap, v), writes=writes)

    def rstd_ops(ssb, n, reads_extra=()):
        act(ssb[:, 1:2], ssb[:, 0:1], AF.Ln, [], [ssb], scale=1.0 / n, bias=EPS)
        act(ssb[:, 2:3], ssb[:, 1:2], AF.Exp, [], [ssb], scale=-0.5)

    dma("sp", cf[:], cf_d[:, :], [], [cf], cf.slot)
    dma("sp", cb[:], cb_d[:, :], [], [cb], cb.slot)
    Wi = [Buf(None, P.slot()) for _ in range(2)]
    Wr = [Buf(None, P.slot()) for _ in range(2)]
    n = 0
    for r in range(8):
        for c0 in range(0, 9216, 2048):
            c1 = min(9216, c0 + 2048)
            dma("pool", winb[r * 128:(r + 1) * 128, c0:c1], w_in[r * 128:(r + 1) * 128, c0:c1], [], [Wi[n % 2]], Wi[n % 2].slot)
            n += 1
    for r in range(8):
        dma("pool", woutb[r * 128:(r + 1) * 128, :], w_out[r * 128:(r + 1) * 128, :], [], [Wr[n % 2]], Wr[n % 2].slot)
        n += 1
        for c0 in range(0, 4096, 2048):
            dma("pool", wupb[r * 128:(r + 1) * 128, c0:c0 + 2048], w_up[r * 128:(r + 1) * 128, c0:c0 + 2048], [], [Wr[n % 2]],
                Wr[n % 2].slot)
            n += 1
    for r in range(32):
        dma("pool", wdnb[r * 128:(r + 1) * 128, :], w_dn[r * 128:(r + 1) * 128, :], [], [Wr[n % 2]], Wr[n % 2].slot)
        n += 1
    P.add("dve", lambda e: e.tensor_scalar(out=cf[:, C_QW:C_QW + 128], in0=cf[:, C_QW:C_QW + 128],
                                           scalar1=float(128 ** -0.5), scalar2=None, op0=ALU.mult),
          writes=[cf])

    winb_v = winb.rearrange("(kc p) n -> p kc n", p=128)
    woutb_v = woutb.rearrange("(kc p) n -> p kc n", p=128)
    worder = []
    wst = {"i": 0, "issued": 0}

    def wissue(k):
        kind, h = worder[k]
        buf = WBs[k % 2]
        if kind == "B":
            for j, c0 in enumerate([h * 256, 1024 + h * 256, 2048 + h * 256, 3072 + h * 256, 7168 + h * 256]):
                dma("sp", buf[:, :, j * 256:(j + 1) * 256], winb_v[:, :, c0:c0 + 256], Wi, [buf], buf.slot)
        elif kind == "C":
            for j, c0 in enumerate([4096, 5120, 6144, 8192]):
                dma("sp", buf[:, :, j * 128:(j + 1) * 128], winb_v[:, :, c0 + h * 128:c0 + (h + 1) * 128], Wi, [buf], buf.slot)
        else:
            dma("sp", buf[:, :, 0:D], woutb_v, Wr, [buf], buf.slot)

    def wget(kind, h):
        k = wst["i"]
        assert worder[k] == (kind, h), (worder[k], kind, h)
        for kk in (k, k + 1):
            if kk < len(worder) and wst["issued"] <= kk:
                wissue(kk)
                wst["issued"] = kk + 1
        wst["i"] += 1
        return WBs[k % 2]
    wupb_v = wupb.rearrange("(kc p) n -> p kc n", p=128)

    def segment(kind, b, t0, ntile, nv, past, pastK, pastV, state, last, pos0):
        xsrc = xp[b] if kind == "p" else xs[b]
        ydst = yp[b] if kind == "p" else ys[b]
        kdst = kp[b] if kind == "p" else ks[b]
        vdst = vp[b] if kind == "p" else vs[b]
        kD = kpD[b] if kind == "p" else Buf(None)
        vD = vpD[b] if kind == "p" else Buf(None)
        ngrp = (ntile + 3) // 4
        gcols = [min(4, ntile - 4 * g) * 128 for g in range(ngrp)]
        dofs = C_DP if kind == "p" else C_DS
        kdofs = C_KDP if kind == "p" else C_KDS
        qdofs = C_QDP if kind == "p" else C_QDS
        gam = gamP if kind == "p" else gamS

        def load_x(ti, dst):
            if kind == "p":
                return dma("sp", dst[:, :], xsrc[t0 + ti * 128:t0 + (ti + 1) * 128, :], [], [dst], dst.slot)
            memset("dve", dst[:], 0.0, [dst])
            return dma("sp", dst[0:nv, :], xsrc[0:nv, :], [], [dst], dst.slot)

        def norm_T(src, nw_col, dstT, c0):
            act(junk[:], src[:], AF.Square, [src], [junk, ssA], accum_out=ssA[:, 0:1])
            rstd_ops(ssA, D)
            stt("dve", hb[:], src[:], ssA[:, 2:3], cf[:, nw_col:nw_col + D], ALU.mult, ALU.mult, [src, ssA, cf], [hb])
            for kc in range(8):
                tr(PST[:, kc * 128:(kc + 1) * 128], hb[:, kc * 128:(kc + 1) * 128], 128, [hb], [PST])
            act(dstT[:, :, c0:c0 + 128], PST[:].rearrange("p (k t) -> p k t", k=8), AF.Copy, [PST], [dstT])

        for ti in range(ntile):
            xt = xb[ti % 2]
            load_x(ti, xt)
            norm_T(xt, C_NMIX, hT[ti // 4], (ti % 4) * 128)

        pend_b = []
        for h in range(4):
            wr = wget("B", h)
            if state == "zero":
                memset("dve", S32[h][:], 0.0, [S32[h]])
                memset("dve", Sbf[h][:], 0.0, [Sbf[h]])
            elif state == "load":
                dma("sp", S32[h][:], st_in[b, h].rearrange("(c p) e -> p c e", p=128), [], [S32[h]], S32[h].slot)
                act(Sbf[h][:], S32[h][:], AF.Copy, [S32[h]], [Sbf[h]])
            for g in range(ngrp):
                nc_ = gcols[g]
                pc = pos0 + g * 512
                dma("sp", csb[:, :, 0:nc_], cs_d[:, :, pc:pc + nc_], [], [csb], csb.slot)
                for (wofs, dstT) in ((0, qTr), (256, kTr)):
                    for hf in range(2):
                        for kc in range(8):
                            mm(PS[hf][:, 0:nc_], wr[:, kc, wofs + hf * 128:wofs + (hf + 1) * 128], hT[g][:, kc, 0:nc_],
                               kc == 0, kc == 7, [wr, hT[g]], [PS[hf]])
                    tt("dve", rt[0][:, 0:nc_], PS[0][:, 0:nc_], csb[:, 0, 0:nc_], ALU.mult, [PS[0], csb], [rt[0]])
                    tt("dve", rt[1][:, 0:nc_], PS[1][:, 0:nc_], csb[:, 1, 0:nc_], ALU.mult, [PS[1], csb], [rt[1]])
                    tt("pool", dstT[:, 0, 0:nc_], rt[0][:, 0:nc_], rt[1][:, 0:nc_], ALU.subtract, [rt[0], rt[1]], [dstT])
                    tt("dve", rt[2][:, 0:nc_], PS[0][:, 0:nc_], csb[:, 1, 0:nc_], ALU.mult, [PS[0], csb], [rt[2]])
                    tt("dve", rt[3][:, 0:nc_], PS[1][:, 0:nc_], csb[:, 0, 0:nc_], ALU.mult, [PS[1], csb], [rt[3]])
                    tt("pool", dstT[:, 1, 0:nc_], rt[2][:, 0:nc_], rt[3][:, 0:nc_], ALU.add, [rt[2], rt[3]], [dstT])
                for t4 in range(nc_ // 128):
                    c0 = t4 * 128
                    for hf in range(2):
                        mm(PS[4][:, 0:128], kTr[:, hf, c0:c0 + 128], qTr[:, hf, c0:c0 + 128], hf == 0, hf == 1,
                           [kTr, qTr], [PS[4]])
                    tt("dve", scd[:], PS[4][:, 0:128], cf[:, dofs + h * 128:dofs + (h + 1) * 128], ALU.mult,
                       [PS[4], cf], [scd])
                    tr(PST[:, 0:128], kTr[:, 0, c0:c0 + 128], 128, [kTr], [PST])
                    tr(PST[:, 128:256], kTr[:, 1, c0:c0 + 128], 128, [kTr], [PST])
                    P.add("dve", (lambda hh: lambda e: e.tensor_scalar(
                        out=k2[:], in0=PST[:, 0:256], scalar1=cf[:, kdofs + hh:kdofs + hh + 1], scalar2=None,
                        op0=ALU.mult))(h), reads=[PST, cf], writes=[k2])
                    for kc in range(8):
                        mm(PS[2][:, 0:512], hT[g][:, kc, c0:c0 + 128], wr[:, kc, 512:1024], kc == 0, kc == 7,
                           [wr, hT[g]], [PS[2]])
                    for kc in range(8):
                        mm(PS[3][:, 0:256], hT[g][:, kc, c0:c0 + 128], wr[:, kc, 1024:1280], kc == 0, kc == 7,
                           [wr, hT[g]], [PS[3]])
                    act(vbf[:], PS[2][:, 0:256], AF.Copy, [PS[2]], [vbf])
                    act(gsi[:], PS[2][:, 256:512], AF.Silu, [PS[2]], [gsi])
                    act(sga[:], PS[3][:, 0:256], AF.Sigmoid, [PS[3]], [sga])
                    tt("pool", gate[:], gsi[:], sga[:], ALU.mult, [gsi, sga], [gate])
                    tt("pool", gate[:], gate[:], cf[:, C_RETW + h * 256:C_RETW + (h + 1) * 256], ALU.mult, [cf], [gate])
                    mm(PS[5][:, 0:256], scd[:], vbf[:], True, False, [scd, vbf], [PS[5]])
                    mm(PS[5][:, 0:256], qTr[:, 0, c0:c0 + 128], Sbf[h][:, 0, :], False, False, [qTr, Sbf[h]], [PS[5]])
                    mm(PS[5][:, 0:256], qTr[:, 1, c0:c0 + 128], Sbf[h][:, 1, :], False, True, [qTr, Sbf[h]], [PS[5]])
                    mm(PS[6][:, 0:256], k2[:, 0:128], vbf[:], True, True, [k2, vbf], [PS[6]])
                    mm(PS[6][:, 256:512], k2[:, 128:256], vbf[:], True, True, [k2, vbf], [PS[6]])
                    while pend_b:
                        pend_b.pop(0)()
                    stt("dve", S32[h][:].rearrange("p c e -> p (c e)"), S32[h][:].rearrange("p c e -> p (c e)"),
                        gam[h], PS[6][:, 0:512], ALU.mult, ALU.add, [PS[6]], [S32[h]])
                    act(Sbf[h][:], S32[h][:], AF.Copy, [S32[h]], [Sbf[h]])
                    act(junk[:, 0:256], PS[5][:, 0:256], AF.Square, [PS[5], cf], [junk, ssB],
                        scale=cf[:, qdofs + h:qdofs + h + 1], accum_out=ssB[:, 0:1])
                    rstd_ops(ssB, 256)
                    tt("dve", ssB[:, 3:4], ssB[:, 2:3], cf[:, qdofs + h:qdofs + h + 1], ALU.mult, [cf], [ssB])
                    rr = rr2[t4 % 2]
                    stt("dve", rr[:], PS[5][:, 0:256], ssB[:, 3:4], gate[:], ALU.mult, ALU.mult, [PS[5], ssB, gate], [rr])

                    def tail(rr=rr, g=g, h=h, c0=c0):
                        tr(PST[:, 256:384], rr[:, 0:128], 128, [rr], [PST])
                        tr(PST[:, 384:512], rr[:, 128:256], 128, [rr], [PST])
                        act(mixT[g][:, 2 * h:2 * h + 2, c0:c0 + 128], PST[:, 256:512].rearrange("p (k t) -> p k t", k=2),
                            AF.Copy, [PST], [mixT[g]])
                    pend_b.append(tail)
                while pend_b:
                    pend_b.pop(0)()
            if last:
                so = (sp_o if kind == "p" else ss_o)[b, h].rearrange("(c p) e -> p c e", p=128)
                dma("act", so, S32[h][:], [S32[h]], [], S32[h].slot)

        npb = past // 128
        for h in range(8):
            w4h = wget("C", h)
            if kind == "p" and past > 0:
                dma("sp", kTh[:, 0:past], kTs[b, h, :, 0:past], [kTsD[b]], [kTh], kTh.slot)
                dma("sp", vh[:, 0:npb, :], vTs[b, h, :, 0:npb, :], [vTsD[b]], [vh], vh.slot)
            for j in range(npb // 4 if kind == "s" else 0):
                ksj, vsj = kst[j % 2], vst[j % 2]
                dma("sp", ksj[:], pastK[j * 512:(j + 1) * 512, h * 128:(h + 1) * 128].rearrange("(b p) d -> p b d", p=128),
                    [kD], [ksj], ksj.slot)
                dma("sp", vsj[:], pastV[j * 512:(j + 1) * 512, h * 128:(h + 1) * 128].rearrange("(b p) d -> p b d", p=128),
                    [vD], [vsj], vsj.slot)
                act(kstb[:], ksj[:], AF.Copy, [ksj], [kstb])
                P.add("pool", (lambda jj, vv: lambda e: e.tensor_copy(out=vh[:, jj * 4:(jj + 1) * 4, :], in_=vv[:]))(j, vsj),
                      reads=[vsj], writes=[vh])
                for bb in range(4):
                    tr(PST[:, bb * 128:(bb + 1) * 128], kstb[:, bb, :], 128, [kstb], [PST])
                P.add("dve", (lambda jj: lambda e: e.tensor_copy(out=kTh[:, jj * 512:(jj + 1) * 512], in_=PST[:, 0:512]))(j),
                      reads=[PST], writes=[kTh])
            def c1_mm(ti):
                g, c0 = ti // 4, (ti % 4) * 128
                pb = PS[0] if ti % 2 == 0 else PS[2]
                for kc in range(8):
                    mm(pb[:, 0:384], hT[g][:, kc, c0:c0 + 128], w4h[:, kc, 0:384], kc == 0, kc == 7, [w4h, hT[g]], [pb])

            def c1_chain(ti):
                rows = 128 if kind == "p" else nv
                pb = PS[0] if ti % 2 == 0 else PS[2]
                p2 = ti % 2
                ko, vo, sc_, qn_, knb_ = kout[p2], vout[p2], ssC2[p2], qn2[p2], knb2[p2]
                act(junk[:, 0:128], pb[:, 0:128], AF.Square, [pb], [junk, sc_], accum_out=sc_[:, 0:1])
                act(junk[:, 128:256], pb[:, 128:256], AF.Square, [pb], [junk, sc_], accum_out=sc_[:, 1:2])
                act(sc_[:, 0:2], sc_[:, 0:2], AF.Ln, [], [sc_], scale=1.0 / 128, bias=EPS)
                act(sc_[:, 2:4], sc_[:, 0:2], AF.Exp, [], [sc_], scale=-0.5)
                stt("dve", qn_[:], pb[:, 0:128], sc_[:, 2:3], cf[:, C_QW:C_QW + 128], ALU.mult, ALU.mult, [pb, sc_, cf], [qn_])
                stt("dve", ko[:], pb[:, 128:256], sc_[:, 3:4], cf[:, C_KW:C_KW + 128], ALU.mult, ALU.mult, [pb, sc_, cf], [ko])
                act(knb_[:], ko[:], AF.Copy, [ko], [knb_])
                act(vo[:], pb[:, 256:384], AF.Copy, [pb], [vo])
                P.add("pool", (lambda tb, vv: lambda e: e.tensor_copy(out=vh[:, tb, :], in_=vv[:]))(npb + ti, vo),
                      reads=[vo], writes=[vh])
                r0 = t0 + ti * 128
                dma("act", kdst[r0:r0 + rows, h * 128:(h + 1) * 128], ko[0:rows, :], [ko], [kD], ko.slot)
                dma("act", vdst[r0:r0 + rows, h * 128:(h + 1) * 128], vo[0:rows, :], [vo], [vD], vo.slot)

            def c1_tr(ti):
                p2 = ti % 2
                qn_, knb_ = qn2[p2], knb2[p2]
                tr(PST[:, 512:640], qn_[:], 128, [qn_], [PST])
                tr(PST[:, 640:768], knb_[:], 128, [knb_], [PST])
                P.add("dve", (lambda cc: lambda e: e.tensor_copy(out=qTh[:, cc:cc + 128], in_=PST[:, 512:640]))(ti * 128),
                      reads=[PST], writes=[qTh])
                P.add("dve", (lambda cc: lambda e: e.tensor_copy(out=kTh[:, cc:cc + 128], in_=PST[:, 640:768]))(past + ti * 128),
                      reads=[PST], writes=[kTh])

            c1_mm(0)
            for ti in range(ntile):
                if ti + 1 < ntile:
                    c1_mm(ti + 1)
                c1_chain(ti)
                c1_tr(ti)
            if kind == "p" and t0 + ntile * 128 < T:
                dma("act", kTs[b, h, :, t0:t0 + ntile * 128], kTh[:, past:past + ntile * 128], [kTh], [kTsD[b]], kTh.slot)
                dma("act", vTs[b, h, :, npb:npb + ntile, :], vh[:, npb:npb + ntile, :], [vh], [vTsD[b]], vh.slot)
            for g in range(ngrp):
                nc_ = gcols[g]
                for kc in range(8):
                    mm(PS[1][:, 0:nc_], w4h[:, kc, 384:512], hT[g][:, kc, 0:nc_], kc == 0, kc == 7, [w4h, hT[g]], [PS[1]])
                act(sgbT[:, g * 512:g * 512 + nc_], PS[1][:, 0:nc_], AF.Sigmoid, [PS[1]], [sgbT])
            its = []
            for g in range(ngrp):
                nq = gcols[g] if kind == "p" else nv
                nqb = (nq + 127) // 128
                kb_hi = npb + g * 4 + nqb - 1
                first = True
                for kb in range(kb_hi, -1, -1):
                    if kb >= npb + g * 4:
                        r = kb - (npb + g * 4)
                        col_lo, diag = r * 128, True
                    else:
                        col_lo, diag = 0, False
                    n = nq - col_lo
                    ksz = 128 if (kind == "p" or kb < npb) else nv
                    its.append(dict(g=g, kb=kb, col_lo=col_lo, diag=diag, n=n, ks=ksz, nd=min(128, n), q0=g * 512 + col_lo,
                                    first=first, last=(kb == 0), nq=nq))
                    first = False
            NI = len(its)
            Zp, Xp, Yp, Op = [PS[0], PS[1]], [PS[2], PS[3]], PS[4], [PS[5], PS[6]]

            def zgroup(dst, it, final_stop):
                ks_, n, nd = it["ks"], it["n"], it["nd"]
                kap = kTh[:, it["kb"] * 128:it["kb"] * 128 + ks_]
                qap = qTh[:, it["q0"]:it["q0"] + n]
                mm(dst[0:ks_, 0:n], kap, qap, True, final_stop and not it["diag"], [kTh, qTh], [dst])
                if it["diag"]:
                    mm(dst[0:ks_, 0:nd], cb[0:ks_, B_ID:B_ID + ks_], cb[0:ks_, B_MNEG:B_MNEG + nd], False, final_stop,
                       [cb], [dst])

            for s in range(NI + 2):
                if s < NI:
                    it = its[s]
                    zgroup(Zp[s % 2], it, True)
                if 0 <= s - 1 < NI:
                    i = s - 1
                    it = its[i]
                    ks_, n = it["ks"], it["n"]
                    zgroup(Xp[i % 2], it, False)
                    mm(Xp[i % 2][0:ks_, 0:n], cb[0:ks_, B_TRI:B_TRI + ks_], SPb[i % 2][0:ks_, 0:n], False, True,
                       [cb, SPb[i % 2]], [Xp[i % 2]])
                    mm(Yp[:, 0:n], cb[0:ks_, B_NEG:B_NEG + 128], SPb[i % 2][0:ks_, 0:n], True, True, [cb, SPb[i % 2]], [Yp])
                    if it["first"]:
                        memset("dve", Csum[:], 0.0, [Csum])
                    cl = it["col_lo"]
                    tt("dve", ARG[i % 2][0:ks_, 0:n], Xp[i % 2][0:ks_, 0:n], Csum[0:ks_, cl:cl + n], ALU.add,
                       [Xp[i % 2], Csum], [ARG[i % 2]])
                    tt("dve", Csum[:, cl:cl + n], Csum[:, cl:cl + n], Yp[:, 0:n], ALU.add, [Yp], [Csum])
                if s < NI:
                    it = its[s]
                    ks_, n = it["ks"], it["n"]
                    act(Eb[s % 2][0:ks_, 0:n], Zp[s % 2][0:ks_, 0:n], AF.Exp, [Zp[s % 2]], [Eb[s % 2]])
                    act(SPb[s % 2][0:ks_, 0:n], Eb[s % 2][0:ks_, 0:n], AF.Ln, [Eb[s % 2]], [SPb[s % 2]], bias=1.0)
                if 0 <= s - 1 < NI:
                    i = s - 1
                    it = its[i]
                    ks_, n = it["ks"], it["n"]
                    act(Wb[i % 2][0:ks_, 0:n], ARG[i % 2][0:ks_, 0:n], AF.Exp, [ARG[i % 2]], [Wb[i % 2]])
                if 0 <= s - 2 < NI:
                    i = s - 2
                    it = its[i]
                    ks_, n, g, cl = it["ks"], it["n"], it["g"], it["col_lo"]
                    opb = Op[g % 2]
                    mm(opb[:, cl:cl + n], vh[0:ks_, it["kb"], :], Wb[i % 2][0:ks_, 0:n], it["first"], it["last"],
                       [vh, Wb[i % 2]], [opb], skip=True)
                    if it["last"]:
                        nq = it["nq"]
                        tt("dve", otmp[:, 0:nq], opb[:, 0:nq], sgbT[:, g * 512:g * 512 + nq], ALU.mult, [opb, sgbT], [otmp])
                        tt("pool", mixT[g][:, h, 0:nq], mixT[g][:, h, 0:nq], otmp[:, 0:nq], ALU.add, [otmp], [mixT[g]])

        wout = wget("D", 0)
        for g in range(ngrp):
            nc_ = gcols[g]
            nt = nc_ // 128
            def d_proj(t4):
                ti = g * 4 + t4
                c0 = t4 * 128
                xt = xb[ti % 2]
                load_x(ti, xt)
                for hf in range(2):
                    pb = PS[(t4 % 2) * 2 + hf]
                    for kc in range(8):
                        mm(pb[:, 0:512], mixT[g][:, kc, c0:c0 + 128], wout[:, kc, hf * 512:(hf + 1) * 512], kc == 0, kc == 7,
                           [mixT[g], wout], [pb])
                    tt("dve", y2[t4][:, hf * 512:(hf + 1) * 512], pb[:, 0:512], xt[:, hf * 512:(hf + 1) * 512], ALU.add,
                       [pb, xt], [y2[t4]])

            d_proj(0)
            for t4 in range(nt):
                if t4 + 1 < nt:
                    d_proj(t4 + 1)
                norm_T(y2[t4], C_NMLP, hT[g], t4 * 128)
            h2T = hT[g]
            def d_load(fcg):
                wu, wd = wup[fcg % 2], wdn[fcg % 2]
                dma("sp", wu[:], wupb_v[:, :, fcg * 256:(fcg + 1) * 256], Wr, [wu], wu.slot)
                dma("sp", wd[:], wdnb[fcg * 256:(fcg + 1) * 256, :].rearrange("(j p) n -> p j n", p=128), Wr, [wd], wd.slot)

            def d_up(fcg):
                wu = wup[fcg % 2]
                ut = uT[fcg % 2]
                for j in range(2):
                    U = PS[2 + j % 2]
                    sq = sqb[j % 2]
                    for kc in range(8):
                        mm(U[:, 0:nc_], wu[:, kc, j * 128:(j + 1) * 128], h2T[:, kc, 0:nc_], kc == 0, kc == 7, [wu, h2T], [U])
                    act(sq[:, 0:nc_], U[:, 0:nc_], AF.Square, [U], [sq])
                    stt("dve", ut[:, j, 0:nc_], U[:, 0:nc_], 0.0, sq[:, 0:nc_], ALU.is_gt, ALU.mult, [U, sq], [ut])

            def d_down(fcg):
                wd = wdn[fcg % 2]
                ut = uT[fcg % 2]
                cnt = 0
                for t4 in range(nt):
                    c0 = t4 * 128
                    for hf in range(2):
                        Y = PS[4 + cnt % 2]
                        cnt += 1
                        for j in range(2):
                            mm(Y[:, 0:512], ut[:, j, c0:c0 + 128], wd[:, j, hf * 512:(hf + 1) * 512], j == 0, j == 1, [ut, wd], [Y])
                        tt("dve", y2[t4][:, hf * 512:(hf + 1) * 512], y2[t4][:, hf * 512:(hf + 1) * 512], Y[:, 0:512], ALU.add,
                           [Y], [y2[t4]])

            d_load(0)
            d_load(1)
            d_up(0)
            for fcg in range(16):
                if fcg + 1 < 16:
                    d_up(fcg + 1)
                d_down(fcg)
                if fcg + 2 < 16:
                    d_load(fcg + 2)
            for t4 in range(nt):
                ti = g * 4 + t4
                rows = 128 if kind == "p" else nv
                r0 = t0 + ti * 128
                dma("act", ydst[r0:r0 + rows, :], y2[t4][0:rows, :], [y2[t4]], [], y2[t4].slot)

    cf_tmp, cb_tmp, cs_tmp, gamP, gamS = make_consts(T, PAST, LS)
    for _ in range(NP * NSEG + NS):
        worder.extend([("B", h) for h in range(4)] + [("C", h) for h in range(8)] + [("D", 0)])
    for b in range(NP):
        for sg in range(NSEG):
            t0 = sg * SEG
            segment("p", b, t0, SEG // 128, 128, t0, kp[b], vp[b], "zero" if sg == 0 else "carry", sg == NSEG - 1, t0)
    for b in range(NS):
        segment("s", b, 0, 1, LS, PAST, ck[b], cv[b], "load", True, T)
    P.add("sp", None, extra=[("d", s, s.n) for s in P.slots if s.n > 0])
    P.emit()
    return nc, stack


_CACHE = {}


def _get(NP, T, NS, PAST, LS, SEG):
    key = (NP, T, NS, PAST, LS, SEG)
    if key not in _CACHE:
        _CACHE[key] = build(*key)
    return _CACHE[key][0]


def run(inputs, ncores, SEG):
    x_prompt = np.asarray(inputs["x_prompt"], np.float32)
    x_sample = np.asarray(inputs["x_sample"], np.float32)
    B, T, _ = x_prompt.shape
    BS, LS, _ = x_sample.shape
    ck = np.asarray(inputs["cache_sb_k"], np.float32)[0]
    cv = np.asarray(inputs["cache_sb_v"], np.float32)[0]
    PAST = ck.shape[1]
    st = np.asarray(inputs["state_ret"], np.float32)[0]
    NP, NS = B // ncores, BS // ncores
    nc = _get(NP, T, NS, PAST, LS, SEG)
    cf, cb, cs, _, _ = make_consts(T, PAST, LS)
    bc = lambda v, n: np.broadcast_to(np.asarray(v, np.float32).reshape(1, -1), (128, n))
    cf[:, C_NMIX:C_NMIX + D] = bc(inputs["norm_mix_w"][0], D)
    cf[:, C_NMLP:C_NMLP + D] = bc(inputs["norm_mlp_w"][0], D)
    cf[:, C_RETW:C_RETW + D] = bc(inputs["ret_norm_w"][0], D)
    cf[:, C_QW:C_QW + 128] = bc(inputs["sb_q_norm_w"][0], 128)
    cf[:, C_KW:C_KW + 128] = bc(inputs["sb_k_norm_w"][0], 128)
    shared = dict(
        w_in=np.ascontiguousarray(inputs["w_in"][0], np.float32),
        w_out=np.ascontiguousarray(inputs["w_out"][0], np.float32),
        w_up=np.ascontiguousarray(inputs["w_up"][0], np.float32),
        w_dn=np.ascontiguousarray(inputs["w_down"][0], np.float32),
        cf=cf, cb=cb, cs=np.ascontiguousarray(cs),
    )
    in_maps = []
    for c in range(ncores):
        m = dict(shared)
        m["xp"] = np.ascontiguousarray(x_prompt[c * NP:(c + 1) * NP])
        m["xs"] = np.ascontiguousarray(x_sample[c * NS:(c + 1) * NS])
        m["ck"] = np.ascontiguousarray(ck[c * NS:(c + 1) * NS].reshape(NS, PAST, D))
        m["cv"] = np.ascontiguousarray(cv[c * NS:(c + 1) * NS].reshape(NS, PAST, D))
        m["st"] = np.ascontiguousarray(st[c * NS:(c + 1) * NS])
        in_maps.append(m)
    res = run_bass_kernel_spmd(nc, in_maps, core_ids=list(range(ncores)))
    R = res.results
    cat = lambda k: np.concatenate([np.asarray(r[k], np.float32) for r in R], axis=0)
    y_p = cat("yp")
    y_s = cat("ys")
    k_p = cat("kp").reshape(1, B, T, 8, 128)
    v_p = cat("vp").reshape(1, B, T, 8, 128)
    s_p = cat("sp")[None]
    k_s = cat("ks").reshape(1, BS, LS, 8, 128)
    v_s = cat("vs").reshape(1, BS, LS, 8, 128)
    s_s = cat("ss")[None]
    return (y_p, y_s, k_p, v_p, s_p, k_s, v_s, s_s)


def kernel(**inputs):
    return run(inputs, NCORES, 1024)
```

```python
import numpy as np
import ml_dtypes
from contextlib import ExitStack
import concourse.bass as bass
import concourse.mybir as mybir
from concourse.bass_utils import run_bass_kernel_spmd

F32 = mybir.dt.float32
BF16 = mybir.dt.bfloat16
AF = mybir.ActivationFunctionType
ALU = mybir.AluOpType
BF = ml_dtypes.bfloat16

D = 1024
EPS = 1e-6
NCORES = 8
ENGS = ["pe", "act", "dve", "pool", "sp"]
EPOCH = 20000

C_NMIX, C_NMLP, C_RETW, C_QW, C_KW = 0, 1024, 2048, 3072, 3200
C_DP, C_DS, C_KDP, C_QDP, C_KDS, C_QDS = 3328, 3840, 4352, 4356, 4360, 4364
NCF = 4368
B_ID, B_TRI, B_NEG, B_MNEG = 0, 128, 256, 384
NCB = 512


class Slot:
    def __init__(self, h):
        self.h = h
        self.n = 0


class Buf:
    def __init__(self, t, slot=None):
        self.t = t
        self.w = None
        self.r = {}
        self.slot = slot

    def __getitem__(self, k):
        return self.t[k]


class Plan:
    def __init__(self, nc, stack):
        self.nc = nc
        self.stack = stack
        self.ops = {e: [] for e in ENGS}
        self.slots = []

    def slot(self):
        h = self.stack.enter_context(self.nc.semaphore(f"dq{len(self.slots)}"))
        s = Slot(h)
        self.slots.append(s)
        return s

    def add(self, eng, fn, reads=(), writes=(), slot=None, extra=()):
        deps = list(extra)
        for b in reads:
            if b.w is not None:
                deps.append(b.w)
            if getattr(b, "psum", False):
                for key, tkr in b.r.items():
                    if key != eng:
                        deps.append(tkr)
        for b in writes:
            if b.w is not None:
                deps.append(b.w)
            deps.extend(b.r.values())
        idx = len(self.ops[eng])
        if slot is not None:
            slot.n += 16
            assert slot.n < 60000
            tk = ("d", slot, slot.n)
        else:
            tk = ("e", eng, idx)
        for b in reads:
            b.r[tk[1]] = tk
        for b in writes:
            b.w = tk
            b.r = {}
        self.ops[eng].append([fn, deps, slot, False])
        return tk

    def emit(self):
        nc = self.nc
        for e in ENGS:
            for op in self.ops[e]:
                for d in op[1]:
                    if d[0] == "e" and not (d[1] == e and e == "pe"):
                        self.ops[d[1]][d[2]][3] = True
        val = {}
        nsem = {}
        for e in ENGS:
            c = 0
            val[e] = []
            for op in self.ops[e]:
                if op[3]:
                    c += 1
                val[e].append(c)
            nsem[e] = max(1, (c + EPOCH - 1) // EPOCH)
        sems = {e: [self.stack.enter_context(nc.semaphore(f"s_{e}{k}")) for k in range(nsem[e])]
                for e in ENGS}
        ops = self.ops

        def run2(e, eng):
            waited = {}
            for i, op in enumerate(ops[e]):
                for d in op[1]:
                    if d[0] == "e":
                        if d[1] == e and e == "pe":
                            continue
                        c = val[d[1]][d[2]]
                        key = d[1]
                        if waited.get(key, 0) >= c:
                            continue
                        waited[key] = c
                        eng.wait_ge(sems[d[1]][(c - 1) // EPOCH], (c - 1) % EPOCH + 1)
                    else:
                        key = id(d[1])
                        if waited.get(key, 0) >= d[2]:
                            continue
                        waited[key] = d[2]
                        eng.wait_ge(d[1].h, d[2])
                if op[0] is None:
                    continue
                ins = op[0](eng)
                if op[2] is not None:
                    ins.then_inc(op[2].h, 16)
                elif op[3]:
                    c = val[e][i]
                    ins.then_inc(sems[e][(c - 1) // EPOCH], 1)

        with nc.Block() as block:
            @block.tensor
            def _(t):
                run2("pe", t)

            @block.scalar
            def _(t):
                run2("act", t)

            @block.vector
            def _(t):
                run2("dve", t)

            @block.gpsimd
            def _(t):
                run2("pool", t)

            @block.sync
            def _(t):
                run2("sp", t)


def make_consts(T, PAST, LS):
    cf = np.zeros((128, NCF), np.float32)
    lg = np.log1p(-np.exp2(-5.0 - np.arange(4, dtype=np.float64)))
    i = np.arange(128)
    for h in range(4):
        g = lg[h]
        Dm = np.zeros((128, 128))
        for jj in range(128):
            for ii in range(128):
                cj, ci = jj // 64, ii // 64
                if cj == ci:
                    Dm[jj, ii] = np.exp(g * abs(ii - jj))
                elif cj < ci:
                    Dm[jj, ii] = np.exp(g * (ii - jj))
        qd = np.exp(g * (i + 1.0))
        cf[:, C_DP + h * 128:C_DP + (h + 1) * 128] = Dm / qd[None, :] / 16.0
        cf[:, C_KDP + h] = np.exp(g * (127.0 - i)) / 16.0
        cf[:, C_QDP + h] = qd
        Ds = np.zeros((128, 128))
        Ds[:LS, :LS] = np.exp(g * np.abs(i[:LS, None] - i[None, :LS]))
        qds = np.ones(128)
        qds[:LS] = np.exp(g * (i[:LS] + 1.0))
        cf[:, C_DS + h * 128:C_DS + (h + 1) * 128] = Ds / qds[None, :] / 16.0
        kds = np.zeros(128)
        kds[:LS] = np.exp(g * (LS - 1.0 - i[:LS])) / 16.0
        cf[:, C_KDS + h] = kds
        cf[:, C_QDS + h] = qds
    gamP = [float(np.exp(lg[h] * 128.0)) for h in range(4)]
    gamS = [float(np.exp(lg[h] * LS)) for h in range(4)]
    cb = np.zeros((128, NCB), np.float32)
    cb[:, B_ID:B_ID + 128] = np.eye(128)
    cb[:, B_TRI:B_TRI + 128] = -(i[:, None] >= i[None, :]).astype(np.float32)
    cb[:, B_NEG:B_NEG + 128] = -1.0
    cb[:, B_MNEG:B_MNEG + 128] = np.where(i[:, None] < i[None, :], 0.0, -30000.0)
    half = 128
    inv_freq = (10000.0 ** (-np.arange(half, dtype=np.float32) / half)).astype(np.float32)
    pos = np.concatenate([np.arange(T), PAST + np.arange(128)]).astype(np.float32)
    ang = (pos[None, :] * inv_freq[:, None]).astype(np.float32)
    cs = np.stack([np.cos(ang), np.sin(ang)], axis=1).astype(np.float32)
    return cf, cb.astype(BF), cs, gamP, gamS


def build(NP, T, NS, PAST, LS, SEG):
    nc = bass.Bass("TRN2", target_bir_lowering=False)
    stack = ExitStack()
    P = Plan(nc, stack)
    NG = SEG // 512
    NSEG = T // SEG
    TT = T + 128

    def dram(name, shape, dtype=F32, kind="ExternalInput"):
        return nc.dram_tensor(name, shape, dtype, kind=kind).ap()

    xp = dram("xp", [NP, T, D])
    xs = dram("xs", [NS, LS, D])
    ck = dram("ck", [NS, PAST, D])
    cv = dram("cv", [NS, PAST, D])
    st_in = dram("st", [NS, 4, 256, 256])
    w_in = dram("w_in", [D, 9216])
    w_out = dram("w_out", [D, D])
    w_up = dram("w_up", [D, 4096])
    w_dn = dram("w_dn", [4096, D])
    cf_d = dram("cf", [128, NCF])
    cb_d = dram("cb", [128, NCB], BF16)
    cs_d = dram("cs", [128, 2, TT])
    yp = dram("yp", [NP, T, D], kind="ExternalOutput")
    ys = dram("ys", [NS, LS, D], kind="ExternalOutput")
    kp = dram("kp", [NP, T, D], kind="ExternalOutput")
    vp = dram("vp", [NP, T, D], kind="ExternalOutput")
    sp_o = dram("sp", [NP, 4, 256, 256], kind="ExternalOutput")
    ks = dram("ks", [NS, LS, D], kind="ExternalOutput")
    vs = dram("vs", [NS, LS, D], kind="ExternalOutput")
    ss_o = dram("ss", [NS, 4, 256, 256], kind="ExternalOutput")
    winb = nc.dram_tensor("winb", [D, 9216], BF16).ap()
    woutb = nc.dram_tensor("woutb", [D, D], BF16).ap()
    wupb = nc.dram_tensor("wupb", [D, 4096], BF16).ap()
    wdnb = nc.dram_tensor("wdnb", [4096, D], BF16).ap()
    kTs = nc.dram_tensor("kTs", [NP, 8, 128, T], BF16).ap()
    vTs = nc.dram_tensor("vTs", [NP, 8, 128, T // 128, 128], BF16).ap()

    def SB(name, shape, dtype, slot=False):
        t = stack.enter_context(nc.sbuf_tensor("sb_" + name, shape, dtype))
        return Buf(t, P.slot() if slot else None)

    def PSB(name, shape, dtype):
        bb = Buf(stack.enter_context(nc.psum_tensor("pp_" + name, shape, dtype)))
        bb.psum = True
        return bb

    cf = SB("cf", [128, NCF], F32, True)
    cb = SB("cb", [128, NCB], BF16, True)
    hT = [SB(f"hT{g}", [128, 8, 512], BF16) for g in range(NG)]
    mixT = [SB(f"mx{g}", [128, 8, 512], BF16) for g in range(NG)]
    S32 = [SB(f"S{h}", [128, 2, 256], F32, True) for h in range(4)]
    Sbf = [SB(f"Sb{h}", [128, 2, 256], BF16) for h in range(4)]
    xb = [SB(f"xb{i}", [128, D], F32, True) for i in range(2)]
    hb = SB("hb", [128, D], BF16)
    junk = hb
    ssA = SB("ssA", [128, 4], F32)
    WBs = [SB(f"WB{i}", [128, 8, 1280], BF16, True) for i in range(2)]
    csb = SB("csb", [128, 2, 512], F32, True)
    qTr = SB("qTr", [128, 2, 512], BF16)
    kTr = SB("kTr", [128, 2, 512], BF16)
    k2 = SB("k2", [128, 256], BF16)
    vbf = SB("vbf", [128, 256], BF16)
    gsi = SB("gsi", [128, 256], F32)
    sga = SB("sga", [128, 256], F32)
    gate = SB("gate", [128, 256], F32)
    scd = SB("scd", [128, 128], BF16)
    ssB = SB("ssB", [128, 4], F32)
    rr2 = [SB(f"rr{i}", [128, 256], BF16) for i in range(2)]
    KMAX = max(T, PAST + 128)
    qTh = SB("qTh", [128, SEG], BF16)
    kTh = SB("kTh", [128, KMAX], BF16, True)
    vh = SB("vh", [128, KMAX // 128, 128], BF16, True)
    sgbT = SB("sgbT", [128, SEG], BF16)
    kst = [SB(f"kst{i}", [128, 2, 128], F32, True) for i in range(2)]
    vst = [SB(f"vst{i}", [128, 2, 128], F32, True) for i in range(2)]
    kstb2 = [SB(f"kstb{i}", [128, 2, 128], BF16) for i in range(2)]
    qn2 = [SB(f"qn{i}", [128, 128], BF16) for i in range(2)]
    knb2 = [SB(f"knb{i}", [128, 128], BF16) for i in range(2)]
    ssC2 = [SB(f"ssC{i}", [128, 4], F32) for i in range(2)]
    kout = [SB(f"kout{i}", [128, 128], F32, True) for i in range(2)]
    vout = [SB(f"vout{i}", [128, 128], F32, True) for i in range(2)]
    ssC = SB("ssC", [128, 4], F32)
    Eb = [SB(f"E{i}", [128, 512], F32) for i in range(2)]
    SPb = [SB(f"SP{i}", [128, 512], BF16) for i in range(2)]
    ARG = [SB(f"ARG{i}", [128, 512], F32) for i in range(2)]
    Wb = [SB(f"W{i}", [128, 512], BF16) for i in range(2)]
    Csum = SB("Csum", [128, 512], F32)
    otmp = SB("otmp", [128, 512], F32)
    rt = [Eb[0], Eb[1], ARG[0], ARG[1]]
    y2 = [SB(f"y2{i}", [128, D], F32, True) for i in range(4)]
    wup = [SB(f"wup{i}", [128, 8, 256], BF16, True) for i in range(2)]
    wdn = [SB(f"wdn{i}", [128, 2, D], BF16, True) for i in range(2)]
    uT = [SB(f"uT{i}", [128, 2, 512], BF16) for i in range(2)]
    sqb = [ARG[0], ARG[1]]
    PS = [PSB(f"ps{i}", [128, 512], F32) for i in range(7)]
    PST = PSB("pst", [128, 1024], BF16)
    wsl = P.slot()
    Wd = Buf(None, wsl)
    kpD = [Buf(None) for _ in range(NP)]
    vpD = [Buf(None) for _ in range(NP)]
    kTsD = [Buf(None) for _ in range(NP)]
    vTsD = [Buf(None) for _ in range(NP)]

    ident = lambda n=128: cb[0:n, B_ID:B_ID + n]

    def dma(q, out_ap, in_ap, reads, writes, slot):
        return P.add(q, lambda e: e.dma_start(out=out_ap, in_=in_ap), reads=reads, writes=writes, slot=slot)

    def mm(out_ap, lhsT, rhs, start, stop, reads, writes, skip=False):
        if skip:
            fn = lambda e: e.matmul(out_ap, lhsT=lhsT, rhs=rhs, start=start, stop=stop, skip_group_check=True)
        else:
            fn = lambda e: e.matmul(out_ap, lhsT=lhsT, rhs=rhs, start=start, stop=stop)
        return P.add("pe", fn, reads=reads, writes=writes)

    def tr(out_ap, in_ap, n, reads, writes):
        idn = ident(n)
        return P.add("pe", lambda e: e.transpose(out_ap, in_ap, idn), reads=list(reads) + [cb], writes=writes)

    def act(out_ap, in_ap, func, reads, writes, **kw):
        return P.add("act", lambda e: e.activation(out=out_ap, in_=in_ap, func=func, **kw), reads=reads, writes=writes)

    def stt(eng, out_ap, in0, scalar, in1, op0, op1, reads, writes):
        return P.add(eng, lambda e: e.scalar_tensor_tensor(out=out_ap, in0=in0, scalar=scalar, in1=in1, op0=op0, op1=op1),
                     reads=reads, writes=writes)

    def tt(eng, out_ap, in0, in1, op, reads, writes):
        return P.add(eng, lambda e: e.tensor_tensor(out=out_ap, in0=in0, in1=in1, op=op), reads=reads, writes=writes)

    def memset(eng, ap, v, writes):
        return P.add(eng, lambda e: e.memset(ap, v), writes=writes)

    def rstd_ops(ssb, n, reads_extra=()):
        act(ssb[:, 1:2], ssb[:, 0:1], AF.Ln, [], [ssb], scale=1.0 / n, bias=EPS)
        act(ssb[:, 2:3], ssb[:, 1:2], AF.Exp, [], [ssb], scale=-0.5)

    dma("sp", cf[:], cf_d[:, :], [], [cf], cf.slot)
    dma("sp", cb[:], cb_d[:, :], [], [cb], cb.slot)
    Wi = [Buf(None, P.slot()) for _ in range(2)]
    Wr = [Buf(None, P.slot()) for _ in range(2)]
    n = 0
    for r in range(8):
        for c0 in range(0, 9216, 2048):
            c1 = min(9216, c0 + 2048)
            dma("pool", winb[r * 128:(r + 1) * 128, c0:c1], w_in[r * 128:(r + 1) * 128, c0:c1], [], [Wi[n % 2]], Wi[n % 2].slot)
            n += 1
    for r in range(8):
        dma("pool", woutb[r * 128:(r + 1) * 128, :], w_out[r * 128:(r + 1) * 128, :], [], [Wr[n % 2]], Wr[n % 2].slot)
        n += 1
        for c0 in range(0, 4096, 2048):
            dma("pool", wupb[r * 128:(r + 1) * 128, c0:c0 + 2048], w_up[r * 128:(r + 1) * 128, c0:c0 + 2048], [], [Wr[n % 2]],
                Wr[n % 2].slot)
            n += 1
    for r in range(32):
        dma("pool", wdnb[r * 128:(r + 1) * 128, :], w_dn[r * 128:(r + 1) * 128, :], [], [Wr[n % 2]], Wr[n % 2].slot)
        n += 1
    P.add("dve", lambda e: e.tensor_scalar(out=cf[:, C_QW:C_QW + 128], in0=cf[:, C_QW:C_QW + 128],
                                           scalar1=float(128 ** -0.5), scalar2=None, op0=ALU.mult),
          writes=[cf])

    winb_v = winb.rearrange("(kc p) n -> p kc n", p=128)
    woutb_v = woutb.rearrange("(kc p) n -> p kc n", p=128)
    worder = []
    wst = {"i": 0, "issued": 0}

    def wissue(k):
        kind, h = worder[k]
        buf = WBs[k % 2]
        if kind == "B":
            for j, c0 in enumerate([h * 256, 1024 + h * 256, 2048 + h * 256, 3072 + h * 256, 7168 + h * 256]):
                dma("sp", buf[:, :, j * 256:(j + 1) * 256], winb_v[:, :, c0:c0 + 256], Wi, [buf], buf.slot)
        elif kind == "C":
            for j, c0 in enumerate([4096, 5120, 6144, 8192]):
                dma("sp", buf[:, :, j * 128:(j + 1) * 128], winb_v[:, :, c0 + h * 128:c0 + (h + 1) * 128], Wi, [buf], buf.slot)
        else:
            dma("sp", buf[:, :, 0:D], woutb_v, Wr, [buf], buf.slot)

    def wget(kind, h):
        k = wst["i"]
        assert worder[k] == (kind, h), (worder[k], kind, h)
        for kk in (k, k + 1):
            if kk < len(worder) and wst["issued"] <= kk:
                wissue(kk)
                wst["issued"] = kk + 1
        wst["i"] += 1
        return WBs[k % 2]
    wupb_v = wupb.rearrange("(kc p) n -> p kc n", p=128)

    def segment(kind, b, t0, ntile, nv, past, pastK, pastV, state, last, pos0):
        xsrc = xp[b] if kind == "p" else xs[b]
        ydst = yp[b] if kind == "p" else ys[b]
        kdst = kp[b] if kind == "p" else ks[b]
        vdst = vp[b] if kind == "p" else vs[b]
        kD = kpD[b] if kind == "p" else Buf(None)
        vD = vpD[b] if kind == "p" else Buf(None)
        ngrp = (ntile + 3) // 4
        gcols = [min(4, ntile - 4 * g) * 128 for g in range(ngrp)]
        dofs = C_DP if kind == "p" else C_DS
        kdofs = C_KDP if kind == "p" else C_KDS
        qdofs = C_QDP if kind == "p" else C_QDS
        gam = gamP if kind == "p" else gamS

        def load_x(ti, dst):
            if kind == "p":
                return dma("sp", dst[:, :], xsrc[t0 + ti * 128:t0 + (ti + 1) * 128, :], [], [dst], dst.slot)
            memset("dve", dst[:], 0.0, [dst])
            return dma("sp", dst[0:nv, :], xsrc[0:nv, :], [], [dst], dst.slot)

        def norm_T(src, nw_col, dstT, c0):
            act(junk[:], src[:], AF.Square, [src], [junk, ssA], accum_out=ssA[:, 0:1])
            rstd_ops(ssA, D)
            stt("dve", hb[:], src[:], ssA[:, 2:3], cf[:, nw_col:nw_col + D], ALU.mult, ALU.mult, [src, ssA, cf], [hb])
            for kc in range(8):
                tr(PST[:, kc * 128:(kc + 1) * 128], hb[:, kc * 128:(kc + 1) * 128], 128, [hb], [PST])
            act(dstT[:, :, c0:c0 + 128], PST[:].rearrange("p (k t) -> p k t", k=8), AF.Copy, [PST], [dstT])

        for ti in range(ntile):
            xt = xb[ti % 2]
            load_x(ti, xt)
            norm_T(xt, C_NMIX, hT[ti // 4], (ti % 4) * 128)

        pend_b = []
        for h in range(4):
            wr = wget("B", h)
            if state == "zero":
                memset("dve", S32[h][:], 0.0, [S32[h]])
                memset("dve", Sbf[h][:], 0.0, [Sbf[h]])
            elif state == "load":
                dma("sp", S32[h][:], st_in[b, h].rearrange("(c p) e -> p c e", p=128), [], [S32[h]], S32[h].slot)
                act(Sbf[h][:], S32[h][:], AF.Copy, [S32[h]], [Sbf[h]])
            for g in range(ngrp):
                nc_ = gcols[g]
                pc = pos0 + g * 512
                dma("sp", csb[:, :, 0:nc_], cs_d[:, :, pc:pc + nc_], [], [csb], csb.slot)
                for (wofs, dstT) in ((0, qTr), (256, kTr)):
                    for hf in range(2):
                        for kc in range(8):
                            mm(PS[hf][:, 0:nc_], wr[:, kc, wofs + hf * 128:wofs + (hf + 1) * 128], hT[g][:, kc, 0:nc_],
                               kc == 0, kc == 7, [wr, hT[g]], [PS[hf]])
                    tt("dve", rt[0][:, 0:nc_], PS[0][:, 0:nc_], csb[:, 0, 0:nc_], ALU.mult, [PS[0], csb], [rt[0]])
                    tt("dve", rt[1][:, 0:nc_], PS[1][:, 0:nc_], csb[:, 1, 0:nc_], ALU.mult, [PS[1], csb], [rt[1]])
                    tt("pool", dstT[:, 0, 0:nc_], rt[0][:, 0:nc_], rt[1][:, 0:nc_], ALU.subtract, [rt[0], rt[1]], [dstT])
                    tt("dve", rt[2][:, 0:nc_], PS[0][:, 0:nc_], csb[:, 1, 0:nc_], ALU.mult, [PS[0], csb], [rt[2]])
                    tt("dve", rt[3][:, 0:nc_], PS[1][:, 0:nc_], csb[:, 0, 0:nc_], ALU.mult, [PS[1], csb], [rt[3]])
                    tt("pool", dstT[:, 1, 0:nc_], rt[2][:, 0:nc_], rt[3][:, 0:nc_], ALU.add, [rt[2], rt[3]], [dstT])
                for t4 in range(nc_ // 128):
                    c0 = t4 * 128
                    for hf in range(2):
                        mm(PS[4][:, 0:128], kTr[:, hf, c0:c0 + 128], qTr[:, hf, c0:c0 + 128], hf == 0, hf == 1,
                           [kTr, qTr], [PS[4]])
                    tt("dve", scd[:], PS[4][:, 0:128], cf[:, dofs + h * 128:dofs + (h + 1) * 128], ALU.mult,
                       [PS[4], cf], [scd])
                    tr(PST[:, 0:128], kTr[:, 0, c0:c0 + 128], 128, [kTr], [PST])
                    tr(PST[:, 128:256], kTr[:, 1, c0:c0 + 128], 128, [kTr], [PST])
                    P.add("dve", (lambda hh: lambda e: e.tensor_scalar(
                        out=k2[:], in0=PST[:, 0:256], scalar1=cf[:, kdofs + hh:kdofs + hh + 1], scalar2=None,
                        op0=ALU.mult))(h), reads=[PST, cf], writes=[k2])
                    for kc in range(8):
                        mm(PS[2][:, 0:512], hT[g][:, kc, c0:c0 + 128], wr[:, kc, 512:1024], kc == 0, kc == 7,
                           [wr, hT[g]], [PS[2]])
                    for kc in range(8):
                        mm(PS[3][:, 0:256], hT[g][:, kc, c0:c0 + 128], wr[:, kc, 1024:1280], kc == 0, kc == 7,
                           [wr, hT[g]], [PS[3]])
                    act(vbf[:], PS[2][:, 0:256], AF.Copy, [PS[2]], [vbf])
                    act(gsi[:], PS[2][:, 256:512], AF.Silu, [PS[2]], [gsi])
                    act(sga[:], PS[3][:, 0:256], AF.Sigmoid, [PS[3]], [sga])
                    tt("pool", gate[:], gsi[:], sga[:], ALU.mult, [gsi, sga], [gate])
                    tt("pool", gate[:], gate[:], cf[:, C_RETW + h * 256:C_RETW + (h + 1) * 256], ALU.mult, [cf], [gate])
                    mm(PS[5][:, 0:256], scd[:], vbf[:], True, False, [scd, vbf], [PS[5]])
                    mm(PS[5][:, 0:256], qTr[:, 0, c0:c0 + 128], Sbf[h][:, 0, :], False, False, [qTr, Sbf[h]], [PS[5]])
                    mm(PS[5][:, 0:256], qTr[:, 1, c0:c0 + 128], Sbf[h][:, 1, :], False, True, [qTr, Sbf[h]], [PS[5]])
                    mm(PS[6][:, 0:256], k2[:, 0:128], vbf[:], True, True, [k2, vbf], [PS[6]])
                    mm(PS[6][:, 256:512], k2[:, 128:256], vbf[:], True, True, [k2, vbf], [PS[6]])
                    while pend_b:
                        pend_b.pop(0)()
                    stt("dve", S32[h][:].rearrange("p c e -> p (c e)"), S32[h][:].rearrange("p c e -> p (c e)"),
                        gam[h], PS[6][:, 0:512], ALU.mult, ALU.add, [PS[6]], [S32[h]])
                    act(Sbf[h][:], S32[h][:], AF.Copy, [S32[h]], [Sbf[h]])
                    act(junk[:, 0:256], PS[5][:, 0:256], AF.Square, [PS[5], cf], [junk, ssB],
                        scale=cf[:, qdofs + h:qdofs + h + 1], accum_out=ssB[:, 0:1])
                    rstd_ops(ssB, 256)
                    tt("dve", ssB[:, 3:4], ssB[:, 2:3], cf[:, qdofs + h:qdofs + h + 1], ALU.mult, [cf], [ssB])
                    rr = rr2[t4 % 2]
                    stt("dve", rr[:], PS[5][:, 0:256], ssB[:, 3:4], gate[:], ALU.mult, ALU.mult, [PS[5], ssB, gate], [rr])

                    def tail(rr=rr, g=g, h=h, c0=c0):
                        tr(PST[:, 256:384], rr[:, 0:128], 128, [rr], [PST])
                        tr(PST[:, 384:512], rr[:, 128:256], 128, [rr], [PST])
                        act(mixT[g][:, 2 * h:2 * h + 2, c0:c0 + 128], PST[:, 256:512].rearrange("p (k t) -> p k t", k=2),
                            AF.Copy, [PST], [mixT[g]])
                    pend_b.append(tail)
                while pend_b:
                    pend_b.pop(0)()
            if last:
                so = (sp_o if kind == "p" else ss_o)[b, h].rearrange("(c p) e -> p c e", p=128)
                dma("act", so, S32[h][:], [S32[h]], [], S32[h].slot)

        npb = past // 128
        for h in range(8):
            w4h = wget("C", h)
            if kind == "p" and past > 0:
                dma("sp", kTh[:, 0:past], kTs[b, h, :, 0:past], [kTsD[b]], [kTh], kTh.slot)
                dma("sp", vh[:, 0:npb, :], vTs[b, h, :, 0:npb, :], [vTsD[b]], [vh], vh.slot)
            for j in range(npb // 2 if kind == "s" else 0):
                ksj, vsj, kbj = kst[j % 2], vst[j % 2], kstb2[j % 2]
                dma("sp", ksj[:], pastK[j * 256:(j + 1) * 256, h * 128:(h + 1) * 128].rearrange("(b p) d -> p b d", p=128),
                    [kD], [ksj], ksj.slot)
                dma("sp", vsj[:], pastV[j * 256:(j + 1) * 256, h * 128:(h + 1) * 128].rearrange("(b p) d -> p b d", p=128),
                    [vD], [vsj], vsj.slot)
                act(kbj[:], ksj[:], AF.Copy, [ksj], [kbj])
                P.add("pool", (lambda jj, vv: lambda e: e.tensor_copy(out=vh[:, jj * 2:(jj + 1) * 2, :], in_=vv[:]))(j, vsj),
                      reads=[vsj], writes=[vh])
                for bb in range(2):
                    tr(PST[:, bb * 128:(bb + 1) * 128], kbj[:, bb, :], 128, [kbj], [PST])
                P.add("dve", (lambda jj: lambda e: e.tensor_copy(out=kTh[:, jj * 256:(jj + 1) * 256], in_=PST[:, 0:256]))(j),
                      reads=[PST], writes=[kTh])
            def c1_mm(ti):
                g, c0 = ti // 4, (ti % 4) * 128
                pb = PS[0] if ti % 2 == 0 else PS[2]
                for kc in range(8):
                    mm(pb[:, 0:384], hT[g][:, kc, c0:c0 + 128], w4h[:, kc, 0:384], kc == 0, kc == 7, [w4h, hT[g]], [pb])

            def c1_chain(ti):
                rows = 128 if kind == "p" else nv
                pb = PS[0] if ti % 2 == 0 else PS[2]
                p2 = ti % 2
                ko, vo, sc_, qn_, knb_ = kout[p2], vout[p2], ssC2[p2], qn2[p2], knb2[p2]
                act(junk[:, 0:128], pb[:, 0:128], AF.Square, [pb], [junk, sc_], accum_out=sc_[:, 0:1])
                act(junk[:, 128:256], pb[:, 128:256], AF.Square, [pb], [junk, sc_], accum_out=sc_[:, 1:2])
                act(sc_[:, 0:2], sc_[:, 0:2], AF.Ln, [], [sc_], scale=1.0 / 128, bias=EPS)
                act(sc_[:, 2:4], sc_[:, 0:2], AF.Exp, [], [sc_], scale=-0.5)
                stt("dve", qn_[:], pb[:, 0:128], sc_[:, 2:3], cf[:, C_QW:C_QW + 128], ALU.mult, ALU.mult, [pb, sc_, cf], [qn_])
                stt("dve", ko[:], pb[:, 128:256], sc_[:, 3:4], cf[:, C_KW:C_KW + 128], ALU.mult, ALU.mult, [pb, sc_, cf], [ko])
                act(knb_[:], ko[:], AF.Copy, [ko], [knb_])
                act(vo[:], pb[:, 256:384], AF.Copy, [pb], [vo])
                P.add("pool", (lambda tb, vv: lambda e: e.tensor_copy(out=vh[:, tb, :], in_=vv[:]))(npb + ti, vo),
                      reads=[vo], writes=[vh])
                r0 = t0 + ti * 128
                dma("act", kdst[r0:r0 + rows, h * 128:(h + 1) * 128], ko[0:rows, :], [ko], [kD], ko.slot)
                dma("act", vdst[r0:r0 + rows, h * 128:(h + 1) * 128], vo[0:rows, :], [vo], [vD], vo.slot)

            def c1_tr(ti):
                p2 = ti % 2
                qn_, knb_ = qn2[p2], knb2[p2]
                tr(PST[:, 512:640], qn_[:], 128, [qn_], [PST])
                tr(PST[:, 640:768], knb_[:], 128, [knb_], [PST])
                P.add("dve", (lambda cc: lambda e: e.tensor_copy(out=qTh[:, cc:cc + 128], in_=PST[:, 512:640]))(ti * 128),
                      reads=[PST], writes=[qTh])
                P.add("dve", (lambda cc: lambda e: e.tensor_copy(out=kTh[:, cc:cc + 128], in_=PST[:, 640:768]))(past + ti * 128),
                      reads=[PST], writes=[kTh])

            c1_mm(0)
            for ti in range(ntile):
                if ti + 1 < ntile:
                    c1_mm(ti + 1)
                c1_chain(ti)
                c1_tr(ti)
            if kind == "p" and t0 + ntile * 128 < T:
                dma("act", kTs[b, h, :, t0:t0 + ntile * 128], kTh[:, past:past + ntile * 128], [kTh], [kTsD[b]], kTh.slot)
                dma("act", vTs[b, h, :, npb:npb + ntile, :], vh[:, npb:npb + ntile, :], [vh], [vTsD[b]], vh.slot)
            for g in range(ngrp):
                nc_ = gcols[g]
                for kc in range(8):
                    mm(PS[1][:, 0:nc_], w4h[:, kc, 384:512], hT[g][:, kc, 0:nc_], kc == 0, kc == 7, [w4h, hT[g]], [PS[1]])
                act(sgbT[:, g * 512:g * 512 + nc_], PS[1][:, 0:nc_], AF.Sigmoid, [PS[1]], [sgbT])
            its = []
            for g in range(ngrp):
                nq = gcols[g] if kind == "p" else nv
                nqb = (nq + 127) // 128
                kb_hi = npb + g * 4 + nqb - 1
                first = True
                for kb in range(kb_hi, -1, -1):
                    if kb >= npb + g * 4:
                        r = kb - (npb + g * 4)
                        col_lo, diag = r * 128, True
                    else:
                        col_lo, diag = 0, False
                    n = nq - col_lo
                    ksz = 128 if (kind == "p" or kb < npb) else nv
                    its.append(dict(g=g, kb=kb, col_lo=col_lo, diag=diag, n=n, ks=ksz, nd=min(128, n), q0=g * 512 + col_lo,
                                    first=first, last=(kb == 0), nq=nq))
                    first = False
            NI = len(its)
            Zp, Xp, Yp, Op = [PS[0], PS[1]], [PS[2], PS[3]], PS[4], [PS[5], PS[6]]

            def zgroup(dst, it, final_stop):
                ks_, n, nd = it["ks"], it["n"], it["nd"]
                kap = kTh[:, it["kb"] * 128:it["kb"] * 128 + ks_]
                qap = qTh[:, it["q0"]:it["q0"] + n]
                mm(dst[0:ks_, 0:n], kap, qap, True, final_stop and not it["diag"], [kTh, qTh], [dst])
                if it["diag"]:
                    mm(dst[0:ks_, 0:nd], cb[0:ks_, B_ID:B_ID + ks_], cb[0:ks_, B_MNEG:B_MNEG + nd], False, final_stop,
                       [cb], [dst])

            for s in range(NI + 2):
                if s < NI:
                    it = its[s]
                    zgroup(Zp[s % 2], it, True)
                if 0 <= s - 1 < NI:
                    i = s - 1
                    it = its[i]
                    ks_, n = it["ks"], it["n"]
                    zgroup(Xp[i % 2], it, False)
                    mm(Xp[i % 2][0:ks_, 0:n], cb[0:ks_, B_TRI:B_TRI + ks_], SPb[i % 2][0:ks_, 0:n], False, True,
                       [cb, SPb[i % 2]], [Xp[i % 2]])
                    mm(Yp[:, 0:n], cb[0:ks_, B_NEG:B_NEG + 128], SPb[i % 2][0:ks_, 0:n], True, True, [cb, SPb[i % 2]], [Yp])
                    if it["first"]:
                        memset("dve", Csum[:], 0.0, [Csum])
                    cl = it["col_lo"]
                    tt("dve", ARG[i % 2][0:ks_, 0:n], Xp[i % 2][0:ks_, 0:n], Csum[0:ks_, cl:cl + n], ALU.add,
                       [Xp[i % 2], Csum], [ARG[i % 2]])
                    tt("dve", Csum[:, cl:cl + n], Csum[:, cl:cl + n], Yp[:, 0:n], ALU.add, [Yp], [Csum])
                if s < NI:
                    it = its[s]
                    ks_, n = it["ks"], it["n"]
                    act(Eb[s % 2][0:ks_, 0:n], Zp[s % 2][0:ks_, 0:n], AF.Exp, [Zp[s % 2]], [Eb[s % 2]])
                    act(SPb[s % 2][0:ks_, 0:n], Eb[s % 2][0:ks_, 0:n], AF.Ln, [Eb[s % 2]], [SPb[s % 2]], bias=1.0)
                if 0 <= s - 1 < NI:
                    i = s - 1
                    it = its[i]
                    ks_, n = it["ks"], it["n"]
                    act(Wb[i % 2][0:ks_, 0:n], ARG[i % 2][0:ks_, 0:n], AF.Exp, [ARG[i % 2]], [Wb[i % 2]])
                if 0 <= s - 2 < NI:
                    i = s - 2
                    it = its[i]
                    ks_, n, g, cl = it["ks"], it["n"], it["g"], it["col_lo"]
                    opb = Op[g % 2]
                    mm(opb[:, cl:cl + n], vh[0:ks_, it["kb"], :], Wb[i % 2][0:ks_, 0:n], it["first"], it["last"],
                       [vh, Wb[i % 2]], [opb], skip=True)
                    if it["last"]:
                        nq = it["nq"]
                        tt("dve", otmp[:, 0:nq], opb[:, 0:nq], sgbT[:, g * 512:g * 512 + nq], ALU.mult, [opb, sgbT], [otmp])
                        tt("pool", mixT[g][:, h, 0:nq], mixT[g][:, h, 0:nq], otmp[:, 0:nq], ALU.add, [otmp], [mixT[g]])

        wout = wget("D", 0)
        for g in range(ngrp):
            nc_ = gcols[g]
            nt = nc_ // 128
            def d_proj(t4):
                ti = g * 4 + t4
                c0 = t4 * 128
                xt = xb[ti % 2]
                load_x(ti, xt)
                for hf in range(2):
                    pb = PS[(t4 % 2) * 2 + hf]
                    for kc in range(8):
                        mm(pb[:, 0:512], mixT[g][:, kc, c0:c0 + 128], wout[:, kc, hf * 512:(hf + 1) * 512], kc == 0, kc == 7,
                           [mixT[g], wout], [pb])
                    tt("dve", y2[t4][:, hf * 512:(hf + 1) * 512], pb[:, 0:512], xt[:, hf * 512:(hf + 1) * 512], ALU.add,
                       [pb, xt], [y2[t4]])

            d_proj(0)
            for t4 in range(nt):
                if t4 + 1 < nt:
                    d_proj(t4 + 1)
                norm_T(y2[t4], C_NMLP, hT[g], t4 * 128)
            h2T = hT[g]
            def d_load(fcg):
                wu, wd = wup[fcg % 2], wdn[fcg % 2]
                dma("sp", wu[:], wupb_v[:, :, fcg * 256:(fcg + 1) * 256], Wr, [wu], wu.slot)
                dma("sp", wd[:], wdnb[fcg * 256:(fcg + 1) * 256, :].rearrange("(j p) n -> p j n", p=128), Wr, [wd], wd.slot)

            def d_up(fcg):
                wu = wup[fcg % 2]
                ut = uT[fcg % 2]
                for j in range(2):
                    U = PS[2 + j % 2]
                    sq = sqb[j % 2]
                    for kc in range(8):
                        mm(U[:, 0:nc_], wu[:, kc, j * 128:(j + 1) * 128], h2T[:, kc, 0:nc_], kc == 0, kc == 7, [wu, h2T], [U])
                    act(sq[:, 0:nc_], U[:, 0:nc_], AF.Square, [U], [sq])
                    stt("dve", ut[:, j, 0:nc_], U[:, 0:nc_], 0.0, sq[:, 0:nc_], ALU.is_gt, ALU.mult, [U, sq], [ut])

            def d_down(fcg):
                wd = wdn[fcg % 2]
                ut = uT[fcg % 2]
                cnt = 0
                for t4 in range(nt):
                    c0 = t4 * 128
                    for hf in range(2):
                        Y = PS[4 + cnt % 2]
                        cnt += 1
                        for j in range(2):
                            mm(Y[:, 0:512], ut[:, j, c0:c0 + 128], wd[:, j, hf * 512:(hf + 1) * 512], j == 0, j == 1, [ut, wd], [Y])
                        tt("dve", y2[t4][:, hf * 512:(hf + 1) * 512], y2[t4][:, hf * 512:(hf + 1) * 512], Y[:, 0:512], ALU.add,
                           [Y], [y2[t4]])

            d_load(0)
            d_load(1)
            d_up(0)
            for fcg in range(16):
                if fcg + 1 < 16:
                    d_up(fcg + 1)
                d_down(fcg)
                if fcg + 2 < 16:
                    d_load(fcg + 2)
            for t4 in range(nt):
                ti = g * 4 + t4
                rows = 128 if kind == "p" else nv
                r0 = t0 + ti * 128
                dma("act", ydst[r0:r0 + rows, :], y2[t4][0:rows, :], [y2[t4]], [], y2[t4].slot)

    cf_tmp, cb_tmp, cs_tmp, gamP, gamS = make_consts(T, PAST, LS)
    for _ in range(NP * NSEG + NS):
        worder.extend([("B", h) for h in range(4)] + [("C", h) for h in range(8)] + [("D", 0)])
    for b in range(NP):
        for sg in range(NSEG):
            t0 = sg * SEG
            segment("p", b, t0, SEG // 128, 128, t0, kp[b], vp[b], "zero" if sg == 0 else "carry", sg == NSEG - 1, t0)
    for b in range(NS):
        segment("s", b, 0, 1, LS, PAST, ck[b], cv[b], "load", True, T)
    P.add("sp", None, extra=[("d", s, s.n) for s in P.slots if s.n > 0])
    P.emit()
    return nc, stack


_CACHE = {}


def _get(NP, T, NS, PAST, LS, SEG):
    key = (NP, T, NS, PAST, LS, SEG)
    if key not in _CACHE:
        _CACHE[key] = build(*key)
    return _CACHE[key][0]


def run(inputs, ncores, SEG):
    x_prompt = np.asarray(inputs["x_prompt"], np.float32)
    x_sample = np.asarray(inputs["x_sample"], np.float32)
    B, T, _ = x_prompt.shape
    BS, LS, _ = x_sample.shape
    ck = np.asarray(inputs["cache_sb_k"], np.float32)[0]
    cv = np.asarray(inputs["cache_sb_v"], np.float32)[0]
    PAST = ck.shape[1]
    st = np.asarray(inputs["state_ret"], np.float32)[0]
    NP, NS = B // ncores, BS // ncores
    nc = _get(NP, T, NS, PAST, LS, SEG)
    cf, cb, cs, _, _ = make_consts(T, PAST, LS)
    bc = lambda v, n: np.broadcast_to(np.asarray(v, np.float32).reshape(1, -1), (128, n))
    cf[:, C_NMIX:C_NMIX + D] = bc(inputs["norm_mix_w"][0], D)
    cf[:, C_NMLP:C_NMLP + D] = bc(inputs["norm_mlp_w"][0], D)
    cf[:, C_RETW:C_RETW + D] = bc(inputs["ret_norm_w"][0], D)
    cf[:, C_QW:C_QW + 128] = bc(inputs["sb_q_norm_w"][0], 128)
    cf[:, C_KW:C_KW + 128] = bc(inputs["sb_k_norm_w"][0], 128)
    shared = dict(
        w_in=np.ascontiguousarray(inputs["w_in"][0], np.float32),
        w_out=np.ascontiguousarray(inputs["w_out"][0], np.float32),
        w_up=np.ascontiguousarray(inputs["w_up"][0], np.float32),
        w_dn=np.ascontiguousarray(inputs["w_down"][0], np.float32),
        cf=cf, cb=cb, cs=np.ascontiguousarray(cs),
    )
    in_maps = []
    for c in range(ncores):
        m = dict(shared)
        m["xp"] = np.ascontiguousarray(x_prompt[c * NP:(c + 1) * NP])
        m["xs"] = np.ascontiguousarray(x_sample[c * NS:(c + 1) * NS])
        m["ck"] = np.ascontiguousarray(ck[c * NS:(c + 1) * NS].reshape(NS, PAST, D))
        m["cv"] = np.ascontiguousarray(cv[c * NS:(c + 1) * NS].reshape(NS, PAST, D))
        m["st"] = np.ascontiguousarray(st[c * NS:(c + 1) * NS])
        in_maps.append(m)
    res = run_bass_kernel_spmd(nc, in_maps, core_ids=list(range(ncores)))
    R = res.results
    cat = lambda k: np.concatenate([np.asarray(r[k], np.float32) for r in R], axis=0)
    y_p = cat("yp")
    y_s = cat("ys")
    k_p = cat("kp").reshape(1, B, T, 8, 128)
    v_p = cat("vp").reshape(1, B, T, 8, 128)
    s_p = cat("sp")[None]
    k_s = cat("ks").reshape(1, BS, LS, 8, 128)
    v_s = cat("vs").reshape(1, BS, LS, 8, 128)
    s_s = cat("ss")[None]
    return (y_p, y_s, k_p, v_p, s_p, k_s, v_s, s_s)


def kernel(**inputs):
    return run(inputs, NCORES, 1024)
```
